# Optimizing a Trainium2 kernel written in Bass

```python
import jax, jax.numpy as jnp
from jax import lax
import numpy as np

D_MODEL = 1024
BATCH = 8
SEQ = 2048
DEPTH = 2
DEC_BATCH = 128
DEC_SEQ = 1
PAST_LEN = 16384
PAGE_SIZE = 128

N_MIXERS = 2
N_RET_LAYERS = (DEPTH + N_MIXERS - 1) // N_MIXERS
N_LRU_LAYERS = DEPTH // N_MIXERS
RET_HEADS = 4
RET_DK = D_MODEL // RET_HEADS
RET_DV = 2 * D_MODEL // RET_HEADS
RET_CHUNK = 128
ROPE_BASE = 10000.0
LRU_WIDTH = D_MODEL
LRU_BLOCKS = 4
LRU_BW = LRU_WIDTH // LRU_BLOCKS
LRU_CONV = 4
LRU_C = 8.0
MEM_LEN = 256
XA_HEADS = 4
XA_HD = D_MODEL // XA_HEADS
FFN_DIM = 3 * D_MODEL
FFN_CONV = 3
EPS = 1e-6

kernel_name = 'hybrid_retention_rglru_memxattn_convffn_step'

F32 = jnp.float32


def rmsnorm(x, g):
    xf = x.astype(F32)
    y = xf * lax.rsqrt(jnp.mean(xf * xf, axis=-1, keepdims=True) + EPS)
    return (y * g.astype(F32)).astype(x.dtype)


def causal_dwconv(x, buf, w, b):
    width = w.shape[0]
    L = x.shape[1]
    xp = jnp.concatenate([buf.astype(x.dtype), x], axis=1)
    y = b.astype(x.dtype) + xp[:, 0:L] * w[0]
    for t in range(1, width):
        y = y + xp[:, t:t + L] * w[t]
    return y, xp[:, L:]


def rotary(x, offset):
    L, dh = x.shape[1], x.shape[-1]
    half = dh // 2
    inv = ROPE_BASE ** (-jnp.arange(half, dtype=F32) / half)
    pos = offset + jnp.arange(L, dtype=F32)
    ang = pos[:, None] * inv[None, :]
    cos = jnp.cos(ang)[None, :, None, :]
    sin = jnp.sin(ang)[None, :, None, :]
    x1, x2 = x[..., :half], x[..., half:]
    return jnp.concatenate([x1 * cos - x2 * sin, x1 * sin + x2 * cos], axis=-1)


def retention_scan(q, k, v, s0):
    B, L = q.shape[0], q.shape[1]
    C = RET_CHUNK if L % RET_CHUNK == 0 else L
    n = L // C
    log_g = jnp.log(1.0 - 2.0 ** (-5.0 - jnp.arange(RET_HEADS, dtype=F32)))
    idx = jnp.arange(C, dtype=F32)
    rel = idx[:, None] - idx[None, :]
    intra = jnp.where(rel[None] >= 0, jnp.exp(log_g[:, None, None] * jnp.maximum(rel, 0.0)[None]), 0.0)
    q_dec = jnp.exp(log_g[None, :] * (idx[:, None] + 1.0))
    k_dec = jnp.exp(log_g[None, :] * (C - 1.0 - idx[:, None]))
    chunk_dec = jnp.exp(log_g * C)

    def to_chunks(t):
        return t.reshape(B, n, C, *t.shape[2:]).swapaxes(0, 1)

    def step(s, inp):
        qc, kc, vc = inp
        att = jnp.einsum('bihd,bjhd->bhij', qc, kc) * intra
        o = (jnp.einsum('bhij,bjhe->bihe', att, vc)
             + jnp.einsum('bihd,bhde->bihe', qc * q_dec[:, :, None], s))
        s = s * chunk_dec[:, None, None] + jnp.einsum('bjhd,bjhe->bhde', kc * k_dec[:, :, None], vc)
        return s, o

    s, o = lax.scan(step, s0, (to_chunks(q), to_chunks(k), to_chunks(v)))
    return o.swapaxes(0, 1).reshape(B, L, RET_HEADS, RET_DV), s


def retention_mixer(h, s0, offset, w_in, w_out):
    B, L, _ = h.shape
    nqk = RET_HEADS * RET_DK
    nv = RET_HEADS * RET_DV
    q, k, v, g = jnp.split(h @ w_in, [nqk, 2 * nqk, 2 * nqk + nv], axis=-1)
    q = rotary(q.reshape(B, L, RET_HEADS, RET_DK).astype(F32), offset)
    k = rotary(k.reshape(B, L, RET_HEADS, RET_DK).astype(F32), offset) * (RET_DK ** -0.5)
    v = v.reshape(B, L, RET_HEADS, RET_DV).astype(F32)
    o, s = retention_scan(q, k, v, s0.astype(F32))
    o = o * lax.rsqrt(jnp.mean(o * o, axis=-1, keepdims=True) + EPS)
    o = (jax.nn.silu(g.astype(F32)) * o.reshape(B, L, nv)).astype(h.dtype)
    return o @ w_out, s.astype(s0.dtype)


def block_diag(x, w, b):
    B, L, _ = x.shape
    xb = x.reshape(B, L, LRU_BLOCKS, LRU_BW)
    return (jnp.einsum('blhi,hij->blhj', xb, w) + b).reshape(B, L, LRU_WIDTH)


def linear_recurrence(a, u, h0):
    def comb(l, r):
        return (l[0] * r[0], r[0] * l[1] + r[1])
    a_cum, u_cum = lax.associative_scan(comb, (a, u), axis=1)
    return a_cum * h0[:, None, :] + u_cum


def rglru_mixer(h, h0, conv_buf, w_in, conv_w, conv_b, wa, ba, wx, bx, lam, w_out):
    xb, gb = jnp.split(h @ w_in, 2, axis=-1)
    xc, new_buf = causal_dwconv(xb, conv_buf, conv_w, conv_b)
    r = jax.nn.sigmoid(block_diag(xc, wa, ba).astype(F32))
    i = jax.nn.sigmoid(block_diag(xc, wx, bx).astype(F32))
    log_a = LRU_C * r * jax.nn.log_sigmoid(lam.astype(F32))
    a = jnp.exp(log_a)
    u = jnp.sqrt(-jnp.expm1(2.0 * log_a)) * (i * xc.astype(F32))
    hs = linear_recurrence(a, u, h0.astype(F32))
    y = (jax.nn.gelu(gb.astype(F32)) * hs).astype(h.dtype)
    return y @ w_out, hs[:, -1].astype(h0.dtype), new_buf.astype(conv_buf.dtype)


def memory_kv(mem, g, w_kv):
    B, M, _ = mem.shape
    k, v = jnp.split(rmsnorm(mem, g) @ w_kv, 2, axis=-1)
    return k.reshape(B, M, XA_HEADS, XA_HD), v.reshape(B, M, XA_HEADS, XA_HD)


def cross_attn(h, mk, mv, w_q, w_o):
    B, L, _ = h.shape
    q = (h @ w_q).reshape(B, L, XA_HEADS, XA_HD).astype(F32)
    s = jnp.einsum('blhd,bmhd->bhlm', q, mk.astype(F32)) * (XA_HD ** -0.5)
    p = jax.nn.softmax(s, axis=-1)
    o = jnp.einsum('bhlm,bmhd->blhd', p, mv.astype(F32)).reshape(B, L, XA_HEADS * XA_HD)
    return o.astype(h.dtype) @ w_o


def conv_ffn(h, buf, w_up, conv_w, conv_b, w_down):
    u, g = jnp.split(h @ w_up, 2, axis=-1)
    uc, new_buf = causal_dwconv(u, buf, conv_w, conv_b)
    y = (jax.nn.gelu(uc.astype(F32)) * g.astype(F32)).astype(h.dtype)
    return y @ w_down, new_buf.astype(buf.dtype)


def trunk(x, offset, ret_s, lru_h, lru_conv, ffn_conv, mem_k, mem_v, p):
    new_ret, new_h, new_conv, new_ffn = [], [], [], []
    for i in range(DEPTH):
        j = i // N_MIXERS
        hn = rmsnorm(x, p['norm_mix'][i])
        if i % N_MIXERS == 0:
            y, s = retention_mixer(hn, ret_s[j], offset, p['ret_w_in'][j], p['ret_w_out'][j])
            new_ret.append(s)
        else:
            y, hl, cb = rglru_mixer(hn, lru_h[j], lru_conv[j], p['lru_w_in'][j], p['lru_conv_w'][j],
                                    p['lru_conv_b'][j], p['lru_wa'][j], p['lru_ba'][j], p['lru_wx'][j],
                                    p['lru_bx'][j], p['lru_lambda'][j], p['lru_w_out'][j])
            new_h.append(hl)
            new_conv.append(cb)
        x = x + y
        x = x + cross_attn(rmsnorm(x, p['norm_xa'][i]), mem_k[i], mem_v[i], p['xa_w_q'][i], p['xa_w_o'][i])
        y, fb = conv_ffn(rmsnorm(x, p['norm_ffn'][i]), ffn_conv[i], p['ffn_w_up'][i], p['ffn_conv_w'][i],
                         p['ffn_conv_b'][i], p['ffn_w_down'][i])
        x = x + y
        new_ffn.append(fb)
    x = rmsnorm(x, p['norm_final'])
    return x, jnp.stack(new_ret), jnp.stack(new_h), jnp.stack(new_conv), jnp.stack(new_ffn)


def setup_inputs(seed: int = 0) -> dict:
    key = jax.random.key(seed)
    ks = iter(jax.random.split(key, 40))

    def nrm(shape, scale):
        return jax.random.normal(next(ks), shape, F32) * scale

    def gain(shape):
        return 1.0 + nrm(shape, 0.05)

    D = D_MODEL
    a_target = jax.random.uniform(next(ks), (N_LRU_LAYERS, LRU_WIDTH), F32, minval=0.9, maxval=0.999)
    s_base = a_target ** (1.0 / LRU_C)
    lru_lambda = jnp.log(s_base) - jnp.log1p(-s_base)
    ret_cols = 2 * RET_HEADS * RET_DK + 2 * RET_HEADS * RET_DV
    return {
        'x_prompt': nrm((BATCH, SEQ, D), 1.0),
        'x_sample': nrm((DEC_BATCH, DEC_SEQ, D), 1.0),
        'state_ret': nrm((N_RET_LAYERS, DEC_BATCH, RET_HEADS, RET_DK, RET_DV), 0.1),
        'state_lru_h': nrm((N_LRU_LAYERS, DEC_BATCH, LRU_WIDTH), 0.5),
        'state_lru_conv': nrm((N_LRU_LAYERS, DEC_BATCH, LRU_CONV - 1, LRU_WIDTH), 1.0),
        'state_ffn_conv': nrm((DEPTH, DEC_BATCH, FFN_CONV - 1, FFN_DIM), 1.0),
        'cache_mem_k': nrm((DEPTH, DEC_BATCH, MEM_LEN, XA_HEADS, XA_HD), 1.0),
        'cache_mem_v': nrm((DEPTH, DEC_BATCH, MEM_LEN, XA_HEADS, XA_HD), 1.0),
        'mem_prompt': nrm((BATCH, MEM_LEN, D), 1.0),
        'norm_mix': gain((DEPTH, D)),
        'norm_xa': gain((DEPTH, D)),
        'norm_mem': gain((DEPTH, D)),
        'norm_ffn': gain((DEPTH, D)),
        'norm_final': gain((D,)),
        'ret_w_in': nrm((N_RET_LAYERS, D, ret_cols), D ** -0.5),
        'ret_w_out': nrm((N_RET_LAYERS, RET_HEADS * RET_DV, D), (RET_HEADS * RET_DV) ** -0.5),
        'lru_w_in': nrm((N_LRU_LAYERS, D, 2 * LRU_WIDTH), D ** -0.5),
        'lru_conv_w': nrm((N_LRU_LAYERS, LRU_CONV, LRU_WIDTH), LRU_CONV ** -0.5),
        'lru_conv_b': nrm((N_LRU_LAYERS, LRU_WIDTH), 0.01),
        'lru_wa': nrm((N_LRU_LAYERS, LRU_BLOCKS, LRU_BW, LRU_BW), LRU_BW ** -0.5),
        'lru_ba': nrm((N_LRU_LAYERS, LRU_BLOCKS, LRU_BW), 0.01),
        'lru_wx': nrm((N_LRU_LAYERS, LRU_BLOCKS, LRU_BW, LRU_BW), LRU_BW ** -0.5),
        'lru_bx': nrm((N_LRU_LAYERS, LRU_BLOCKS, LRU_BW), 0.01),
        'lru_lambda': lru_lambda,
        'lru_w_out': nrm((N_LRU_LAYERS, LRU_WIDTH, D), LRU_WIDTH ** -0.5),
        'xa_w_q': nrm((DEPTH, D, XA_HEADS * XA_HD), D ** -0.5),
        'xa_w_kv': nrm((DEPTH, D, 2 * XA_HEADS * XA_HD), D ** -0.5),
        'xa_w_o': nrm((DEPTH, XA_HEADS * XA_HD, D), (XA_HEADS * XA_HD) ** -0.5),
        'ffn_w_up': nrm((DEPTH, D, 2 * FFN_DIM), D ** -0.5),
        'ffn_conv_w': nrm((DEPTH, FFN_CONV, FFN_DIM), FFN_CONV ** -0.5),
        'ffn_conv_b': nrm((DEPTH, FFN_DIM), 0.01),
        'ffn_w_down': nrm((DEPTH, FFN_DIM, D), FFN_DIM ** -0.5),
    }


def reference(x_prompt, x_sample, state_ret, state_lru_h, state_lru_conv, state_ffn_conv,
              cache_mem_k, cache_mem_v, mem_prompt, norm_mix, norm_xa, norm_mem, norm_ffn, norm_final,
              ret_w_in, ret_w_out, lru_w_in, lru_conv_w, lru_conv_b, lru_wa, lru_ba, lru_wx, lru_bx,
              lru_lambda, lru_w_out, xa_w_q, xa_w_kv, xa_w_o, ffn_w_up, ffn_conv_w, ffn_conv_b, ffn_w_down):
    p = {
        'norm_mix': norm_mix, 'norm_xa': norm_xa, 'norm_ffn': norm_ffn, 'norm_final': norm_final,
        'ret_w_in': ret_w_in, 'ret_w_out': ret_w_out,
        'lru_w_in': lru_w_in, 'lru_conv_w': lru_conv_w, 'lru_conv_b': lru_conv_b,
        'lru_wa': lru_wa, 'lru_ba': lru_ba, 'lru_wx': lru_wx, 'lru_bx': lru_bx,
        'lru_lambda': lru_lambda, 'lru_w_out': lru_w_out,
        'xa_w_q': xa_w_q, 'xa_w_o': xa_w_o,
        'ffn_w_up': ffn_w_up, 'ffn_conv_w': ffn_conv_w, 'ffn_conv_b': ffn_conv_b, 'ffn_w_down': ffn_w_down,
    }
    bp = x_prompt.shape[0]
    dt = x_prompt.dtype
    mk_list, mv_list = [], []
    for i in range(DEPTH):
        mk_i, mv_i = memory_kv(mem_prompt, norm_mem[i], xa_w_kv[i])
        mk_list.append(mk_i)
        mv_list.append(mv_i)
    new_mem_k_p = jnp.stack(mk_list)
    new_mem_v_p = jnp.stack(mv_list)
    ret0 = jnp.zeros((N_RET_LAYERS, bp, RET_HEADS, RET_DK, RET_DV), dt)
    h0 = jnp.zeros((N_LRU_LAYERS, bp, LRU_WIDTH), dt)
    lconv0 = jnp.zeros((N_LRU_LAYERS, bp, LRU_CONV - 1, LRU_WIDTH), dt)
    fconv0 = jnp.zeros((DEPTH, bp, FFN_CONV - 1, FFN_DIM), dt)
    y_prompt, new_ret_p, new_lru_h_p, new_lru_conv_p, new_ffn_conv_p = trunk(
        x_prompt, 0, ret0, h0, lconv0, fconv0, new_mem_k_p, new_mem_v_p, p)
    y_sample, new_ret_s, new_lru_h_s, new_lru_conv_s, new_ffn_conv_s = trunk(
        x_sample, PAST_LEN, state_ret, state_lru_h, state_lru_conv, state_ffn_conv,
        cache_mem_k, cache_mem_v, p)
    return (y_prompt, y_sample, new_ret_p, new_ret_s, new_lru_h_p, new_lru_h_s,
            new_lru_conv_p, new_lru_conv_s, new_ffn_conv_p, new_ffn_conv_s, new_mem_k_p, new_mem_v_p)
```

```python
import contextlib
import numpy as np
import concourse.bass as bass
import concourse.mybir as mybir
from concourse.bass_utils import run_bass_kernel_spmd

F32 = mybir.dt.float32
BF16 = mybir.dt.bfloat16
F32R = mybir.dt.float32r
AF = mybir.ActivationFunctionType
ALU = mybir.AluOpType
AX = mybir.AxisListType

P = 128
D = 1024
SEQ = 2048
T = 1024
NT = SEQ // T
NS = 16
TT = T + NS
H = 4
FFN = 3072
MEM = 256
EPS = 1e-6
NSLOT = 5
ENG = ("pe", "act", "dve", "pool", "sp")

V_NMIX, V_NXA, V_NMEM, V_NFFN, V_NFIN = 0, 16, 32, 48, 64
V_LCW, V_LCB, V_LBA, V_LBX, V_LLAM = 72, 104, 112, 120, 128
V_FCW, V_FCB = 136, 280
NV = 328
C_ID, C_MASK, C_KDEC, C_EPSP = 0, 128, 640, 644
NC = 648


import os
_STOP_AFTER = int(os.environ.get("KSTOP", "100000"))
_KSUB = int(os.environ.get("KSUB", "0"))


class _StopBuild(Exception):
    pass


class Res:
    __slots__ = ("w", "r")

    def __init__(self, inherit=()):
        self.w = None
        self.r = list(inherit)


class Op:
    __slots__ = ("fn", "deps", "eng", "idx", "dma", "ms", "msn", "dsem", "dtgt")


class KB:
    ND = int(os.environ.get("KND", "8"))

    def __init__(self, nc):
        self.nc = nc
        self.ops = {e: [] for e in ENG}
        self.dcnt = {e: 0 for e in ENG}
        self.inherit = []
        self.phase_res = []
        self.pres = [Res() for _ in range(8)]
        self.pcur = 0
        self.pinned = set()

    def res(self, arena=False):
        r = Res(self.inherit if arena else ())
        if arena:
            self.phase_res.append(r)
        return r

    def begin_phase(self):
        toks = {}

        def merge(t):
            if t.dma:
                k = ("d",) + t.dsem
                if k not in toks or toks[k].dtgt < t.dtgt:
                    toks[k] = t
            else:
                k = t.eng
                if k not in toks or toks[k].idx < t.idx:
                    toks[k] = t

        for t in self.inherit:
            merge(t)
        for r in self.phase_res:
            if r.w is not None:
                merge(r.w)
            for t in r.r:
                merge(t)
        self.inherit = list(toks.values())
        self.phase_res = []

    def bank(self):
        while self.pcur in self.pinned:
            self.pcur = (self.pcur + 1) % 8
        i = self.pcur
        self.pcur = (self.pcur + 1) % 8
        return i

    def pin(self):
        i = self.bank()
        self.pinned.add(i)
        return i

    def unpin(self, i):
        self.pinned.discard(i)

    def op(self, eng, fn, rd=(), wr=(), dma=False):
        o = Op()
        o.fn = fn
        o.eng = eng
        o.idx = len(self.ops[eng])
        o.dma = dma
        o.ms = False
        o.msn = 0
        deps = {}

        def add(t, war=False):
            if t is None:
                return
            if (not t.dma) and t.eng == eng and (not dma):
                if eng == "pe":
                    return
            deps[id(t)] = t

        for r in rd:
            add(r.w)
        for r in wr:
            add(r.w)
            for t in r.r:
                add(t, war=True)
        o.deps = list(deps.values())
        for t in o.deps:
            if not t.dma:
                t.ms = True
        if dma:
            j = self.dcnt[eng]
            self.dcnt[eng] += 1
            o.dsem = (eng, j % self.ND)
            o.dtgt = 16 * (j // self.ND + 1)
        self.ops[eng].append(o)
        for r in rd:
            r.r.append(o)
        for r in wr:
            r.w = o
            r.r = []
        return o

    def I(self, eng, meth, rd, wr, *a, **k):
        return self.op(eng, lambda e: getattr(e, meth)(*a, **k), rd=rd, wr=wr)

    def dma(self, q, out, in_, rd, wr):
        return self.op(q, lambda e: e.dma_start(out=out, in_=in_), rd=rd, wr=wr, dma=True)

    def emit(self, es):
        nc = self.nc
        sem = {e: es.enter_context(nc.semaphore("s_" + e)) for e in ENG}
        dsem = {}
        for e in ("sp", "pool", "act"):
            for i in range(self.ND):
                dsem[(e, i)] = es.enter_context(nc.semaphore("d_%s%d" % (e, i)))
        for e in ENG:
            c = 0
            for o in self.ops[e]:
                if o.ms and not o.dma:
                    c += 1
                    o.msn = c
        ops = self.ops

        def run(e):
            def body(eng):
                waited = {}

                def wait(key, sm, val):
                    if waited.get(key, 0) < val:
                        eng.wait_ge(sm, val)
                        waited[key] = val

                for o in ops[e]:
                    for t in o.deps:
                        if t.dma:
                            wait(t.dsem, dsem[t.dsem], t.dtgt)
                        else:
                            wait(t.eng, sem[t.eng], t.msn)
                    if o.dma:
                        if o.dtgt > 16:
                            wait(o.dsem, dsem[o.dsem], o.dtgt - 16)
                        ins = o.fn(eng)
                        ins.then_inc(dsem[o.dsem], 16)
                    else:
                        ins = o.fn(eng)
                        if o.ms:
                            ins.then_inc(sem[e], 1)

            return body

        with nc.Block() as block:
            block.tensor(run("pe"))
            block.scalar(run("act"))
            block.vector(run("dve"))
            block.gpsimd(run("pool"))
            block.sync(run("sp"))


def build_program(dbg=None):
    nc = bass.Bass("TRN2", target_bir_lowering=False)
    es = contextlib.ExitStack()
    kb = KB(nc)

    def din(name, shape):
        return nc.dram_tensor(name, list(shape), F32, kind="ExternalInput").ap()

    def dout(name, shape):
        return nc.dram_tensor(name, list(shape), F32, kind="ExternalOutput").ap()

    x_p = din("x_p", [SEQ, D])
    x_s = din("x_s", [NS, D])
    ret_s0 = din("ret_s0", [NS, H, 256, 512])
    lru_h0 = din("lru_h0", [NS, D])
    lru_cv0 = din("lru_cv0", [NS, 3, D])
    ffn_cv0 = din("ffn_cv0", [2, NS, 2, FFN])
    mem_k = din("mem_k", [2, NS, MEM, D])
    mem_v = din("mem_v", [2, NS, MEM, D])
    mem_p = din("mem_p", [MEM, D])
    vec_d = din("vec", [P, NV])
    cst_d = din("cst", [P, NC])
    rope_c = din("rope_c", [NT, P, TT])
    rope_s = din("rope_s", [NT, P, TT])
    ret_w_in = din("ret_w_in", [D, 6144])
    ret_w_out = din("ret_w_out", [2048, D])
    lru_w_in = din("lru_w_in", [D, 2048])
    lru_wa = din("lru_wa", [4, 256, 256])
    lru_wx = din("lru_wx", [4, 256, 256])
    lru_w_out = din("lru_w_out", [D, D])
    xa_w_q = din("xa_w_q", [2, D, D])
    xa_w_kv = din("xa_w_kv", [2, D, 2048])
    xa_w_o = din("xa_w_o", [2, D, D])
    ffn_w_up = din("ffn_w_up", [2, D, 6144])
    ffn_w_down = din("ffn_w_down", [2, FFN, D])

    y_p = dout("y_p", [SEQ, D])
    y_s = dout("y_s", [NS, D])
    o_ret_p = dout("o_ret_p", [H, 256, 512])
    o_ret_s = dout("o_ret_s", [NS, H, 256, 512])
    o_lruh_p = dout("o_lruh_p", [1, D])
    o_lruh_s = dout("o_lruh_s", [NS, D])
    o_lrucv_p = dout("o_lrucv_p", [3, D])
    o_lrucv_s = dout("o_lrucv_s", [NS, 3, D])
    o_ffncv_p = dout("o_ffncv_p", [2, 2, FFN])
    o_ffncv_s = dout("o_ffncv_s", [2, NS, 2, FFN])
    o_memk_p = dout("o_memk_p", [2, MEM, D])
    o_memv_p = dout("o_memv_p", [2, MEM, D])
    out_ops = []

    def sb(name, shape, dt):
        return es.enter_context(nc.sbuf_tensor(name, list(shape), dt))

    ps = [es.enter_context(nc.psum_tensor("ps%d" % i, [P, 512], F32)) for i in range(8)]
    cst = sb("cst_sb", [P, NC], F32)
    vec = sb("vecs", [P, NV], F32)
    idb = sb("idb", [P, P], BF16)
    onesb = sb("onesb", [P, P], BF16)
    c8 = sb("c8", [P, 8], F32)
    xT = sb("xT", [P, 8, TT], F32)
    hn = sb("hn", [P, 8, TT], BF16)
    wsl = [sb("w%d" % i, [P, 4096], BF16) for i in range(NSLOT)]
    S = sb("S", [P, H, 2, 512], F32)
    S_r = [sb("S_r%d" % i, [P, 2, 512], BF16) for i in range(2)]
    kTm = [sb("kTm%d" % l, [P, 8, MEM], BF16) for l in range(2)]
    vtm = [sb("vtm%d" % l, [P, 2, D], BF16) for l in range(2)]
    rstd = [sb("rstd%d" % i, [P, 512], F32) for i in range(2)]
    lru_halo = sb("lru_halo", [P, 8, 3], F32)
    hcarry = sb("hcarry", [P, 8], F32)
    ffn_halo = [sb("ffn_halo%d" % l, [P, 24, 2], F32) for l in range(2)]
    h0s = sb("h0s", [P, 8, NS], F32)
    cvs = sb("cvs", [P, 8, NS, 3], F32)
    fcs = [sb("fcs%d" % l, [P, 24, NS, 2], F32) for l in range(2)]
    xbs_keep = sb("xbs_keep", [P, 8, NS], F32)
    hs_keep = sb("hs_keep", [P, 8, NS], F32)
    ukeep = [sb("ukeep%d" % l, [P, 24, NS], F32) for l in range(2)]
    stat = sb("stat", [P, 64], F32)
    ARENA_W = 13900
    S0b = sb("S0b", [P, 2, 512], F32R)
    qmr = sb("qmr", [P, 2, NS, NS], F32R)
    arena = sb("arena", [P, ARENA_W], F32)

    R = {}

    def rs(key, arena_=False):
        if key not in R:
            R[key] = kb.res(arena_)
        return R[key]

    r_cst, r_vec, r_idb, r_ones, r_c8 = (kb.res() for _ in range(5))
    r_x = [kb.res() for _ in range(3)]
    r_hn = [kb.res() for _ in range(3)]
    r_ws = [kb.res() for _ in range(NSLOT)]
    r_S = [kb.res() for _ in range(H)]
    r_Sr = [kb.res() for _ in range(2)]
    r_kTm = [kb.res() for _ in range(2)]
    r_vtm = [kb.res() for _ in range(2)]
    r_rstd = [kb.res() for _ in range(2)]
    r_lhalo = [kb.res() for _ in range(8)]
    r_hc = [kb.res() for _ in range(8)]
    r_fhalo = [[kb.res() for _ in range(24)] for _ in range(2)]
    r_h0s, r_cvs, r_xbs, r_hsk = (kb.res() for _ in range(4))
    r_fcs = [kb.res() for _ in range(2)]
    r_uk = [kb.res() for _ in range(2)]
    cident = cst[:, C_ID:C_ID + P]

    class Ar:
        off = 0

    def phase():
        kb.begin_phase()
        Ar.off = 0

    def carve(shape, dt):
        n = int(np.prod(shape))
        words = n if dt == F32 else (n + 1) // 2
        words = (words + 7) // 8 * 8
        a = arena[:, Ar.off:Ar.off + words]
        Ar.off += words
        assert Ar.off <= ARENA_W, Ar.off
        if dt != F32:
            a = a.bitcast(dt)[:, 0:n]
        else:
            a = a[:, 0:n]
        if len(shape) == 2:
            a = a.rearrange("p (a b) -> p a b", a=shape[0])
        elif len(shape) == 3:
            a = a.rearrange("p (a b c) -> p a b c", a=shape[0], b=shape[1])
        return a

    def ares():
        return kb.res(True)

    def bankf(i):
        return ps[i][:]

    def bankb(i):
        return ps[i][:].bitcast(BF16)

    def mm(out, pairs, rd, wr, start=True, stop=True):
        def fn(e):
            n = len(pairs)
            ins = None
            for i, (l, r) in enumerate(pairs):
                ins = e.matmul(out, l, r, start=(start and i == 0), stop=(stop and i == n - 1))
            return ins
        return kb.op("pe", fn, rd=rd, wr=wr)

    def trs(items, rd, wr):
        def fn(e):
            ins = None
            for (o, i, idn) in items:
                ins = e.transpose(o, i, idn)
            return ins
        return kb.op("pe", fn, rd=rd, wr=wr)

    def act(out, in_, func, rd, wr, **k):
        return kb.I("act", "activation", rd, wr, out, in_, func, **k)

    def tt(out, a, b, op, rd, wr, eng="dve"):
        return kb.I(eng, "tensor_tensor", rd, wr, out, a, b, op)

    def ts(out, a, s1, s2, op0, op1, rd, wr, **k):
        if op1 is None:
            return kb.I("dve", "tensor_scalar", rd, wr, out, a, s1, None, op0, **k)
        return kb.I("dve", "tensor_scalar", rd, wr, out, a, s1, s2, op0, op1, **k)

    def stt(out, a, s, b, op0, op1, rd, wr):
        return kb.I("dve", "scalar_tensor_tensor", rd, wr, out, a, s, b, op0, op1)

    def vcopy(out, in_, rd, wr):
        return kb.I("dve", "tensor_copy", rd, wr, out, in_)

    def acopy(out, in_, rd, wr, **k):
        return act(out, in_, AF.Copy, rd, wr, **k)

    cp_flip = [0]

    def anycopy(out, in_, rd, wr):
        cp_flip[0] ^= 1
        if cp_flip[0]:
            return acopy(out, in_, rd, wr)
        return vcopy(out, in_, rd, wr)

    def rsqrt_inplace(ap, rd_extra, res, scale, bias):
        act(ap, ap, AF.Sqrt, rd_extra + [res], [res], scale=scale, bias=bias)
        kb.I("dve", "reciprocal", [res], [res], ap, ap)

    class WS:
        def __init__(self):
            self.pieces = []
            self.issued = 0
            self.used = 0
            self.free = list(range(NSLOT))
            self.slot = {}

        def add(self, key, dmas):
            self.pieces.append((key, dmas))

        def _prefetch(self):
            while self.free and self.issued < len(self.pieces):
                key, dmas = self.pieces[self.issued]
                s = self.free.pop(0)
                self.slot[key] = s
                for (dstf, src) in dmas:
                    kb.dma("pool", dstf(wsl[s]), src, [], [r_ws[s]])
                self.issued += 1

        def next(self, key):
            i = self.used
            assert self.pieces[i][0] == key, (self.pieces[i][0], key)
            self._prefetch()
            assert key in self.slot, ("no free weight slot for", key)
            self.used += 1
            s = self.slot[key]
            return wsl[s], r_ws[s]

        def release(self, *keys):
            for key in keys:
                self.free.append(self.slot.pop(key))
            self._prefetch()

    ws = WS()

    def v3(a, b):
        return lambda t: t[:, 0:a * b].rearrange("p (a b) -> p a b", a=a)

    def v3h(hf):
        return lambda t: t[:, 0:4096].rearrange("p (a b) -> p a b", a=4)[:, :, hf * 512:(hf + 1) * 512]

    def wcols(w, c0, nc_):
        return w[:, c0:c0 + nc_].rearrange("(k p) f -> p k f", p=P)

    def wrows(w, r0):
        return w[r0:r0 + 512, :].rearrange("(k p) f -> p k f", p=P)

    for l in range(2):
        for j in range(4):
            ws.add(("kv", l, j), [(v3(8, 512), wcols(xa_w_kv[l], 512 * j, 512))])
    for tt_ in range(NT):
        for h in range(H):
            ws.add(("rq", tt_, h), [(v3(8, 256), wcols(ret_w_in, 256 * h, 256))])
            ws.add(("rk", tt_, h), [(v3(8, 256), wcols(ret_w_in, 1024 + 256 * h, 256))])
            ws.add(("rv", tt_, h), [(v3(8, 512), wcols(ret_w_in, 2048 + 512 * h, 512))])
            ws.add(("rg", tt_, h), [(v3(8, 512), wcols(ret_w_in, 4096 + 512 * h, 512))])
            ws.add(("ro", tt_, h), [(v3h(0), wrows(ret_w_out, 512 * h)[:, :, 0:512]),
                                    (v3h(1), wrows(ret_w_out, 512 * h)[:, :, 512:1024])])
        for l in range(2):
            if l == 1:
                for j in range(4):
                    ws.add(("lx", tt_, j), [(v3(8, 256), wcols(lru_w_in, 256 * j, 256))])
                    ws.add(("lgb", tt_, j), [(v3(8, 256), wcols(lru_w_in, 1024 + 256 * j, 256))])
                for j in range(2):
                    ws.add(("lo", tt_, j), [(v3(8, 512), wcols(lru_w_out, 512 * j, 512))])
            for j in range(2):
                ws.add(("xq", tt_, l, j), [(v3(8, 512), wcols(xa_w_q[l], 512 * j, 512))])
            for j in range(2):
                ws.add(("xo", tt_, l, j), [(v3(8, 512), wcols(xa_w_o[l], 512 * j, 512))])
            for hh in range(2):
                for j3 in range(3):
                    j = 3 * hh + j3
                    ws.add(("fu", tt_, l, j), [(v3(8, 512), wcols(ffn_w_up[l], 512 * j, 512))])
                    ws.add(("fg", tt_, l, j), [(v3(8, 512), wcols(ffn_w_up[l], FFN + 512 * j, 512))])
                for j3 in range(3):
                    j = 3 * hh + j3
                    ws.add(("fd", tt_, l, j), [(v3h(0), wrows(ffn_w_down[l], 512 * j)[:, :, 0:512]),
                                               (v3h(1), wrows(ffn_w_down[l], 512 * j)[:, :, 512:1024])])

    def blocks_of(tt_):
        b = [(0, 512, 0), (512, 512, 1)]
        if tt_ == NT - 1:
            b.append((T, NS, 2))
        return b

    statres = {}

    def st(col, n=P):
        if col not in statres:
            statres[col] = kb.res()
        return stat[0:n, col:col + 1], statres[col]

    class PH:
        sq = None
        r_sq = None
        stg = None
        r_stg = None
        k = 0
        count = 0
        stopped = False

    def phase(stage=False, sq_=True):
        PH.count += 1
        if PH.count > _STOP_AFTER:
            PH.stopped = True
            raise _StopBuild()
        kb.begin_phase()
        Ar.off = 0
        PH.sq = carve([8, 512], BF16) if sq_ else None
        PH.r_sq = ares() if sq_ else None
        if stage:
            PH.stg = [carve([512], F32) for _ in range(2)]
            PH.r_stg = [ares() for _ in range(2)]
        PH.k = 0

    def stage_buf():
        i = PH.k % 2
        PH.k += 1
        return PH.stg[i], PH.r_stg[i]

    def tm_to_fm(src_dram, ntok, nfc, dstf, r_dst):
        for f0 in range(0, nfc, 4):
            nf = min(4, nfc - f0)
            tin, r_in = stage_buf()
            kb.dma("sp", tin[0:ntok, 0:nf * P], src_dram[:, f0 * P:(f0 + nf) * P], [], [r_in])
            b = kb.bank()
            items = [(bankf(b)[:, k * ntok:(k + 1) * ntok], tin[0:ntok, k * P:(k + 1) * P],
                      cident[0:ntok, 0:ntok]) for k in range(nf)]
            trs(items, [r_in, r_cst], [kb.pres[b]])
            anycopy(dstf(f0, nf), bankf(b)[:, 0:nf * ntok].rearrange("p (a b) -> p a b", a=nf),
                    [kb.pres[b]], [r_dst])

    def fm_to_tm(srcf, r_src, ntok, nfc, dst_dram, q="sp"):
        for f0 in range(0, nfc, 4):
            nf = min(4, nfc - f0)
            b = kb.bank()
            items = [(bankf(b)[0:ntok, k * P:(k + 1) * P], srcf(f0 + k), cident) for k in range(nf)]
            trs(items, r_src + [r_cst], [kb.pres[b]])
            tout, r_o = stage_buf()
            anycopy(tout[0:ntok, 0:nf * P], bankf(b)[0:ntok, 0:nf * P], [kb.pres[b]], [r_o])
            out_ops.append(kb.dma(q, dst_dram[:, f0 * P:(f0 + nf) * P], tout[0:ntok, 0:nf * P], [r_o], []))

    def norm(src, r_src_b, gcol, blks, dst, r_dst_b):
        sq, r_sq = PH.sq, PH.r_sq
        for (off, n, bi) in blks:
            act(sq[:, :, 0:n], src[:, :, off:off + n], AF.Square, [r_src_b[bi]], [r_sq])
            b = kb.bank()
            mm(bankf(b)[:, 0:n], [(onesb[:], sq[:, fc, 0:n]) for fc in range(8)], [r_sq, r_ones], [kb.pres[b]])
            rb = bi % 2
            act(rstd[rb][:, 0:n], bankf(b)[:, 0:n], AF.Sqrt, [kb.pres[b]], [r_rstd[rb]], scale=1.0 / D, bias=EPS)
            kb.I("dve", "reciprocal", [r_rstd[rb]], [r_rstd[rb]], rstd[rb][:, 0:n], rstd[rb][:, 0:n])
            for fc in range(8):
                stt(dst[:, fc, off:off + n], src[:, fc, off:off + n], vec[:, gcol + fc:gcol + fc + 1],
                    rstd[rb][:, 0:n], ALU.mult, ALU.mult, [r_src_b[bi], r_rstd[rb], r_vec], [r_dst_b[bi]])

    def lin_fm(r_w, wview, nfo, src, r_src_f, blks, cb, kc_n=8):
        for (off, n, bi) in blks:
            for fo in range(nfo):
                b = kb.bank()
                mm(bankf(b)[:, 0:n], [(wview[:, kc, fo * P:(fo + 1) * P], src[:, kc, off:off + n]) for kc in range(kc_n)],
                   [r_w] + r_src_f(bi), [kb.pres[b]])
                cb(b, fo, off, n, bi)

    def resid_add(b, fc, off, n, bi):
        tt(xT[:, fc, off:off + n], bankf(b)[:, 0:n], xT[:, fc, off:off + n], ALU.add, [kb.pres[b], r_x[bi]], [r_x[bi]])

    kb.dma("sp", cst[:], cst_d, [], [r_cst])
    kb.dma("sp", vec[:], vec_d, [], [r_vec])
    acopy(idb[:], cident, [r_cst], [r_idb])
    kb.I("dve", "memset", [], [r_ones], onesb[:], 1.0)
    act(c8[:], vec[:, V_LLAM:V_LLAM + 8], AF.Sigmoid, [r_vec], [r_c8])
    act(c8[:], c8[:], AF.Ln, [r_c8], [r_c8])
    ts(c8[:], c8[:], 8.0, None, ALU.mult, None, [r_c8], [r_c8])
    for h in range(H):
        kb.I("dve", "memset", [], [r_S[h]], S[:, h], 0.0)
    for fc in range(8):
        kb.I("dve", "memset", [], [r_lhalo[fc]], lru_halo[:, fc, :], 0.0)
        kb.I("dve", "memset", [], [r_hc[fc]], hcarry[:, fc:fc + 1], 0.0)
    for l in range(2):
        kb.I("dve", "memset", [], r_fhalo[l], ffn_halo[l][:], 0.0)

    def record_all():
        phase(stage=True)
        memT = carve([8, MEM], F32)
        mhn = carve([8, MEM], BF16)
        r_memT, r_mhn = ares(), ares()
        for mc in range(2):
            tm_to_fm(mem_p[mc * P:(mc + 1) * P, :], P, 8,
                     lambda f0, n, mc=mc: memT[:, f0:f0 + n, mc * P:(mc + 1) * P], r_memT)
        def sub(n):
            if _KSUB == n:
                PH.stopped = True
                raise _StopBuild()
        sub(1)
        for l in range(2):
            norm(memT, [r_memT], V_NMEM + 8 * l, [(0, MEM, 0)], mhn, [r_mhn])
            sub(2)
            for j in range(4):
                w, r_w = ws.next(("kv", l, j))
                sub(3)
                wv = v3(8, 512)(w)
                if j < 2:
                    def cbk(b, fo, off, n, bi, j=j, l=l):
                        anycopy(kTm[l][:, 4 * j + fo, :], bankf(b)[:, 0:MEM], [kb.pres[b]], [r_kTm[l]])
                    lin_fm(r_w, wv, 4, mhn, lambda bi: [r_mhn], [(0, MEM, 0)], cbk)
                sub(4)
                for mc in range(2):
                    b = kb.bank()
                    mm(bankf(b), [(mhn[:, kc, mc * P:(mc + 1) * P], wv[:, kc, :]) for kc in range(8)],
                       [r_w, r_mhn], [kb.pres[b]])
                    tout, r_o = stage_buf()
                    acopy(tout, bankf(b), [kb.pres[b]], [r_o])
                    sub(5)
                    dst = (o_memk_p if j < 2 else o_memv_p)[l, mc * P:(mc + 1) * P, (j % 2) * 512:(j % 2 + 1) * 512]
                    out_ops.append(kb.dma("sp", dst, tout, [r_o], []))
                    sub(6)
                    if j >= 2:
                        vcopy(vtm[l][:, mc, (j - 2) * 512:(j - 1) * 512], tout, [r_o], [r_vtm[l]])
                sub(100 + 10 * l + j)
                ws.release(("kv", l, j))
                sub(200 + 10 * l + j)

        for tt_ in range(NT):
            blks = blocks_of(tt_)
            pblks = blks[:2]
            last = (tt_ == NT - 1)
            tok0 = tt_ * T

            phase(stage=True, sq_=False)
            for ci in range(8):
                tm_to_fm(x_p[tok0 + ci * P: tok0 + (ci + 1) * P, :], P, 8,
                         lambda f0, n, ci=ci: xT[:, f0:f0 + n, ci * P:(ci + 1) * P], r_x[ci // 4])
            if last:
                tm_to_fm(x_s, NS, 8, lambda f0, n: xT[:, f0:f0 + n, T:TT], r_x[2])
                tm_to_fm(lru_h0, NS, 8, lambda f0, n: h0s[:, f0:f0 + n, :], r_h0s)
                tm_to_fm(lru_cv0.rearrange("b t f -> (b t) f"), NS * 3, 8,
                         lambda f0, n: cvs[:, f0:f0 + n].rearrange("p a b t -> p a (b t)"), r_cvs)
                for l in range(2):
                    tm_to_fm(ffn_cv0[l].rearrange("b t f -> (b t) f"), NS * 2, 24,
                             lambda f0, n, l=l: fcs[l][:, f0:f0 + n].rearrange("p a b t -> p a (b t)"), r_fcs[l])

            for l in range(2):
                phase()
                norm(xT, r_x, V_NMIX + 8 * l, blks, hn, r_hn)
                if l == 0:
                    tmpb = PH.sq.rearrange("p a b -> p (a b)").bitcast(F32)
                    tmp = [tmpb[:, i * 512:(i + 1) * 512] for i in range(4)]
                    r_tmp = PH.r_sq
                    ropec = carve([512], F32)
                    ropes = carve([512], F32)
                    r_rope = ares()
                    qT = carve([2, 512], BF16)
                    kT = carve([2, 512], BF16)
                    k_tm = carve([4, 256], BF16)
                    v_tm = carve([4, 512], BF16)
                    sg = carve([4, 512], BF16)
                    ogT = carve([4, 512], BF16)
                    attT = [carve([P], BF16) for _ in range(2)]
                    og = [carve([512], BF16) for _ in range(2)]
                    junk = carve([512], BF16)
                    r_qT, r_kT, r_ktm, r_vt, r_sg, r_ogT, r_junk = (ares() for _ in range(7))
                    r_att = [ares() for _ in range(2)]
                    r_og = [ares() for _ in range(2)]
                    if last:
                        Snew = [carve([512], F32) for _ in range(2)]
                        kmask = [carve([256], BF16) for _ in range(2)]
                        prod = carve([2, NS], BF16)
                        tcross = carve([512], F32)
                        o_s = carve([512], F32)
                        ogs = carve([512], BF16)
                        r_S0 = ares()
                        r_Sn = [ares() for _ in range(2)]
                        r_qm, r_prod, r_tc, r_os, r_ogs = (ares() for _ in range(5))
                        r_km = [ares() for _ in range(2)]
                    cnt2 = [0]
                    for h in range(H):
                        g_h = 1.0 - 2.0 ** (-5.0 - h)
                        cdec = float(np.float64(g_h) ** 128)
                        wq, r_wq = ws.next(("rq", tt_, h))
                        wk, r_wk = ws.next(("rk", tt_, h))
                        wvv, r_wv = ws.next(("rv", tt_, h))
                        wgg, r_wg = ws.next(("rg", tt_, h))
                        wo, r_wo = ws.next(("ro", tt_, h))
                        wqv, wkv = v3(8, 256)(wq), v3(8, 256)(wk)
                        wvv_, wgv_ = v3(8, 512)(wvv), v3(8, 512)(wgg)
                        wov = v3(4, 1024)(wo)
                        if tt_ == 0:
                            kb.I("dve", "memset", [], [r_Sr[0]], S_r[0][:], 0.0)
                        else:
                            acopy(S_r[0][:], S[:, h], [r_S[h]], [r_Sr[0]])
                        gci = 0
                        for (off, n, bi) in blks:
                            smp = (bi == 2)
                            kb.dma("sp", ropec[:, 0:n], rope_c[tt_, :, off:off + n], [], [r_rope])
                            kb.dma("sp", ropes[:, 0:n], rope_s[tt_, :, off:off + n], [], [r_rope])
                            for (wv_, r_w_, dst, r_dst) in ((wqv, r_wq, qT, r_qT), (wkv, r_wk, kT, r_kT)):
                                bb = []
                                for half in range(2):
                                    b = kb.bank()
                                    mm(bankf(b)[:, 0:n],
                                       [(wv_[:, kc, half * P:(half + 1) * P], hn[:, kc, off:off + n]) for kc in range(8)],
                                       [r_w_, r_hn[bi]], [kb.pres[b]])
                                    bb.append(b)
                                c_ = ropec[:, 0:n]
                                s_ = ropes[:, 0:n]
                                p0 = bankf(bb[0])[:, 0:n]
                                p1 = bankf(bb[1])[:, 0:n]
                                t = [x[:, 0:n] for x in tmp]
                                tt(t[0], p0, c_, ALU.mult, [kb.pres[bb[0]], r_rope], [r_tmp])
                                tt(t[1], p1, s_, ALU.mult, [kb.pres[bb[1]], r_rope], [r_tmp])
                                tt(t[2], p0, s_, ALU.mult, [kb.pres[bb[0]], r_rope], [r_tmp])
                                tt(t[3], p1, c_, ALU.mult, [kb.pres[bb[1]], r_rope], [r_tmp])
                                tt(dst[:, 0, 0:n], t[0], t[1], ALU.subtract, [r_tmp], [r_dst])
                                tt(dst[:, 1, 0:n], t[2], t[3], ALU.add, [r_tmp], [r_dst])
                            b = kb.bank()
                            if not smp:
                                items = []
                                for c4 in range(4):
                                    for half in range(2):
                                        items.append((bankb(b)[:, (c4 * 2 + half) * P:(c4 * 2 + half + 1) * P],
                                                      kT[:, half, c4 * P:(c4 + 1) * P], idb[:]))
                                trs(items, [r_kT, r_idb], [kb.pres[b]])
                                act(k_tm[:, :, :], bankb(b)[:, 0:1024].rearrange("p (a b) -> p a b", a=4), AF.Identity,
                                    [kb.pres[b], r_cst], [r_ktm], scale=cst[:, C_KDEC + h:C_KDEC + h + 1])
                            else:
                                items = [(bankb(b)[0:NS, half * P:(half + 1) * P], kT[:, half, 0:NS], idb[:]) for half in range(2)]
                                trs(items, [r_kT, r_idb], [kb.pres[b]])
                                acopy(k_tm[0:NS, 0, :], bankb(b)[0:NS, 0:256], [kb.pres[b]], [r_ktm], scale=1.0 / 16.0)
                            for (wv_, r_w_, dst, r_dst, fn_) in ((wvv_, r_wv, v_tm, r_vt, AF.Copy),
                                                                 (wgv_, r_wg, sg, r_sg, AF.Silu)):
                                for c4 in range(max(1, n // P)):
                                    m = min(P, n)
                                    b = kb.bank()
                                    mm(bankf(b)[0:m, :],
                                       [(hn[:, kc, off + c4 * P: off + c4 * P + m], wv_[:, kc, :]) for kc in range(8)],
                                       [r_w_, r_hn[bi]], [kb.pres[b]])
                                    act(dst[0:m, c4, :], bankf(b)[0:m, :], fn_, [kb.pres[b]], [r_dst])
                            if not smp:
                                for c4 in range(4):
                                    csl = slice(c4 * P, (c4 + 1) * P)
                                    par = gci % 2
                                    gci += 1
                                    ba = kb.bank()
                                    mm(bankf(ba)[:, 0:P], [(kT[:, half, csl], qT[:, half, csl]) for half in range(2)],
                                       [r_kT, r_qT], [kb.pres[ba]])
                                    tt(attT[par], bankf(ba)[:, 0:P], cst[:, C_MASK + h * P:C_MASK + (h + 1) * P], ALU.mult,
                                       [kb.pres[ba], r_cst], [r_att[par]])
                                    bd = []
                                    for half in range(2):
                                        b = kb.bank()
                                        mm(bankf(b), [(k_tm[:, c4, half * P:(half + 1) * P], v_tm[:, c4, :])],
                                           [r_ktm, r_vt], [kb.pres[b]])
                                        bd.append(b)
                                    bo = kb.bank()
                                    mm(bankf(bo), [(attT[par], v_tm[:, c4, :])] +
                                       [(qT[:, half, csl], S_r[par][:, half, :]) for half in range(2)],
                                       [r_att[par], r_vt, r_qT, r_Sr[par]], [kb.pres[bo]])
                                    for half in range(2):
                                        stt(S[:, h, half, :], S[:, h, half, :], cdec, bankf(bd[half]), ALU.mult, ALU.add,
                                            [kb.pres[bd[half]], r_S[h]], [r_S[h]])
                                    acopy(S_r[1 - par][:], S[:, h], [r_S[h]], [r_Sr[1 - par]])
                                    ssq, r_ssq = st(8 + (cnt2[0] % 4))
                                    cnt2[0] += 1
                                    act(junk, bankf(bo), AF.Square, [kb.pres[bo]], [r_junk, r_ssq], accum_out=ssq)
                                    rsqrt_inplace(ssq, [r_cst], r_ssq, 1.0 / 512.0, cst[:, C_EPSP + h:C_EPSP + h + 1])
                                    stt(og[par], bankf(bo), ssq, sg[:, c4, :], ALU.mult, ALU.mult,
                                        [kb.pres[bo], r_ssq, r_sg], [r_og[par]])
                                    bt = kb.bank()
                                    trs([(bankb(bt)[:, e4 * P:(e4 + 1) * P], og[par][:, e4 * P:(e4 + 1) * P], idb[:]) for e4 in range(4)],
                                        [r_og[par], r_idb], [kb.pres[bt]])
                                    anycopy(ogT[:, :, csl], bankb(bt)[:, 0:512].rearrange("p (a b) -> p a b", a=4),
                                            [kb.pres[bt]], [r_ogT])
                            else:
                                tt(prod, qT[:, :, 0:NS], kT[:, :, 0:NS], ALU.mult, [r_qT, r_kT], [r_prod])
                                b = kb.bank()
                                mm(bankf(b)[0:NS, 0:1], [(prod[:, half, :], onesb[:, 0:1]) for half in range(2)],
                                   [r_prod, r_ones], [kb.pres[b]])
                                att_s, r_atts = st(32, NS)
                                ts(att_s, bankf(b)[0:NS, 0:1], 1.0 / 16.0, None, ALU.mult, None, [kb.pres[b]], [r_atts])
                                ts(qmr[:].rearrange("p a b c -> p (a b c)"), cst[:, 0:2 * NS * NS], 0.0, None, ALU.mult, None,
                                   [r_cst], [r_qm])
                                for half in range(2):
                                    for bb_ in range(NS):
                                        vcopy(qmr[:, half, bb_, bb_:bb_ + 1], qT[:, half, bb_:bb_ + 1], [r_qT], [r_qm])
                                bc = kb.pin()
                                for bb_ in range(NS):
                                    kb.dma("pool", S0b[:],
                                           ret_s0[bb_, h].rearrange("(k p) e -> p k e", p=P), [], [r_S0])
                                    mm(bankf(bc)[0:NS, :],
                                       [(qmr[:, half, bb_, :], S0b[:, half, :]) for half in range(2)],
                                       [r_qm, r_S0], [kb.pres[bc]], start=(bb_ == 0), stop=(bb_ == NS - 1))
                                    sp_ = bb_ % 2
                                    ts(kmask[sp_][0:NS, :], k_tm[0:NS, 0, :], cident[0:NS, bb_:bb_ + 1], None, ALU.mult, None,
                                       [r_ktm, r_cst], [r_km[sp_]])
                                    for half in range(2):
                                        b = kb.bank()
                                        mm(bankf(b), [(kmask[sp_][0:NS, half * P:(half + 1) * P], v_tm[0:NS, 0, :])],
                                           [r_km[sp_], r_vt], [kb.pres[b]])
                                        stt(Snew[half], S0b[:, half, :].bitcast(F32), g_h, bankf(b), ALU.mult, ALU.add,
                                            [kb.pres[b], r_S0], [r_Sn[half]])
                                        out_ops.append(kb.dma("sp", o_ret_s[bb_, h, half * P:(half + 1) * P, :], Snew[half],
                                                              [r_Sn[half]], []))
                                act(tcross[0:NS, :], bankf(bc)[0:NS, :], AF.Copy, [kb.pres[bc]], [r_tc], scale=g_h)
                                kb.unpin(bc)
                                stt(o_s[0:NS, :], v_tm[0:NS, 0, :], att_s, tcross[0:NS, :], ALU.mult, ALU.add,
                                    [r_vt, r_atts, r_tc], [r_os])
                                ssq, r_ssq = st(33, NS)
                                act(junk[0:NS, :], o_s[0:NS, :], AF.Square, [r_os], [r_junk, r_ssq], accum_out=ssq)
                                rsqrt_inplace(ssq, [], r_ssq, 1.0 / 512.0, EPS)
                                stt(ogs[0:NS, :], o_s[0:NS, :], ssq, sg[0:NS, 0, :], ALU.mult, ALU.mult,
                                    [r_os, r_ssq, r_sg], [r_ogs])
                                bt = kb.bank()
                                trs([(bankb(bt)[:, e4 * NS:(e4 + 1) * NS], ogs[0:NS, e4 * P:(e4 + 1) * P], idb[0:NS, 0:NS]) for e4 in range(4)],
                                    [r_ogs, r_idb], [kb.pres[bt]])
                                anycopy(ogT[:, :, 0:NS], bankb(bt)[:, 0:4 * NS].rearrange("p (a b) -> p a b", a=4),
                                        [kb.pres[bt]], [r_ogT])
                            for fc in range(8):
                                b = kb.bank()
                                mm(bankf(b)[:, 0:n], [(wov[:, e4, fc * P:(fc + 1) * P], ogT[:, e4, 0:n]) for e4 in range(4)],
                                   [r_wo, r_ogT], [kb.pres[b]])
                                resid_add(b, fc, off, n, bi)
                        ws.release(("rq", tt_, h), ("rk", tt_, h), ("rv", tt_, h), ("rg", tt_, h), ("ro", tt_, h))
                    if last:
                        out_ops.append(kb.dma("sp", o_ret_p.rearrange("h (k p) e -> p h k e", p=P), S[:], r_S, []))
                else:
                    y = carve([8, TT], BF16)
                    r_y = [ares() for _ in range(3)]
                    xbh = carve([2, 3 + 512], F32)
                    r_xbh = ares()
                    xc = carve([2, 512], F32)
                    xcb = carve([2, 512], BF16)
                    ra = carve([2, 512], F32)
                    iu = carve([2, 512], F32)
                    t1 = carve([2, 512], F32)
                    gel = PH.sq.rearrange("p a b -> p (a b)").bitcast(F32)[:, 0:1024].rearrange("p (a b) -> p a b", a=2)
                    r_gel = PH.r_sq
                    r_xc, r_xcb, r_ra, r_iu, r_t1 = (ares() for _ in range(5))
                    wav = carve([4, 2, 256], BF16)
                    wxv = carve([4, 2, 256], BF16)
                    r_wgt = ares()
                    kb.dma("pool", wav, lru_wa.rearrange("b (k p) o -> p b k o", p=P), [], [r_wgt])
                    kb.dma("pool", wxv, lru_wx.rearrange("b (k p) o -> p b k o", p=P), [], [r_wgt])
                    for j in range(4):
                        wx_, r_wx = ws.next(("lx", tt_, j))
                        wgb_, r_wgb = ws.next(("lgb", tt_, j))
                        wxv_ = v3(8, 256)(wx_)
                        wgbv_ = v3(8, 256)(wgb_)
                        for (off, n, bi) in blks:
                            smp = (bi == 2)
                            for c2 in range(2):
                                fo = 2 * j + c2
                                b = kb.bank()
                                mm(bankf(b)[:, 0:n], [(wxv_[:, kc, c2 * P:(c2 + 1) * P], hn[:, kc, off:off + n]) for kc in range(8)],
                                   [r_wx, r_hn[bi]], [kb.pres[b]])
                                w3 = vec[:, V_LCW + 24 + fo:V_LCW + 24 + fo + 1]
                                w2 = vec[:, V_LCW + 16 + fo:V_LCW + 16 + fo + 1]
                                w1 = vec[:, V_LCW + 8 + fo:V_LCW + 8 + fo + 1]
                                w0 = vec[:, V_LCW + fo:V_LCW + fo + 1]
                                cb_ = vec[:, V_LCB + fo:V_LCB + fo + 1]
                                xco = xc[:, c2, 0:n]
                                if not smp:
                                    acopy(xbh[:, c2, 3:3 + n], bankf(b)[:, 0:n], [kb.pres[b]], [r_xbh])
                                    vcopy(xbh[:, c2, 0:3], lru_halo[:, fo, :], [r_lhalo[fo]], [r_xbh])
                                    vcopy(lru_halo[:, fo, :], xbh[:, c2, n:n + 3], [r_xbh], [r_lhalo[fo]])
                                    act(xco, xbh[:, c2, 3:3 + n], AF.Identity, [r_xbh, r_vec], [r_xc], scale=w3, bias=cb_)
                                    taps = [xbh[:, c2, 2:2 + n], xbh[:, c2, 1:1 + n], xbh[:, c2, 0:n]]
                                    r_taps = r_xbh
                                else:
                                    acopy(xbs_keep[:, fo, :], bankf(b)[:, 0:n], [kb.pres[b]], [r_xbs])
                                    act(xco, bankf(b)[:, 0:n], AF.Identity, [kb.pres[b], r_vec], [r_xc], scale=w3, bias=cb_)
                                    taps = [cvs[:, fo, :, 2], cvs[:, fo, :, 1], cvs[:, fo, :, 0]]
                                    r_taps = r_cvs
                                for tap_ap, wcol in zip(taps, (w2, w1, w0)):
                                    stt(xco, tap_ap, wcol, xco, ALU.mult, ALU.add, [r_taps, r_xc, r_vec], [r_xc])
                                acopy(xcb[:, c2, 0:n], xco, [r_xc], [r_xcb])
                            for c2 in range(2):
                                fo = 2 * j + c2
                                osl = slice(c2 * P, (c2 + 1) * P)
                                br = kb.bank()
                                mm(bankf(br)[:, 0:n], [(wav[:, j, k2, osl], xcb[:, k2, 0:n]) for k2 in range(2)],
                                   [r_wgt, r_xcb], [kb.pres[br]])
                                bi_ = kb.bank()
                                mm(bankf(bi_)[:, 0:n], [(wxv[:, j, k2, osl], xcb[:, k2, 0:n]) for k2 in range(2)],
                                   [r_wgt, r_xcb], [kb.pres[bi_]])
                                bg = kb.bank()
                                mm(bankf(bg)[:, 0:n], [(wgbv_[:, kc, osl], hn[:, kc, off:off + n]) for kc in range(8)],
                                   [r_wgb, r_hn[bi]], [kb.pres[bg]])
                                a_ = ra[:, c2, 0:n]
                                i_ = iu[:, c2, 0:n]
                                t_ = t1[:, c2, 0:n]
                                g_ = gel[:, c2, 0:n]
                                xco = xc[:, c2, 0:n]
                                act(a_, bankf(br)[:, 0:n], AF.Sigmoid, [kb.pres[br], r_vec], [r_ra],
                                    bias=vec[:, V_LBA + fo:V_LBA + fo + 1])
                                act(i_, bankf(bi_)[:, 0:n], AF.Sigmoid, [kb.pres[bi_], r_vec], [r_iu],
                                    bias=vec[:, V_LBX + fo:V_LBX + fo + 1])
                                act(a_, a_, AF.Exp, [r_ra, r_c8], [r_ra], scale=c8[:, fo:fo + 1])
                                act(t_, a_, AF.Square, [r_ra], [r_t1])
                                act(t_, t_, AF.Sqrt, [r_t1], [r_t1], scale=-1.0, bias=1.0)
                                act(g_, bankf(bg)[:, 0:n], AF.Gelu_apprx_tanh, [kb.pres[bg]], [r_gel])
                                tt(i_, i_, xco, ALU.mult, [r_iu, r_xc], [r_iu])
                                tt(i_, i_, t_, ALU.mult, [r_iu, r_t1], [r_iu])
                                if not smp:
                                    kb.I("dve", "tensor_tensor_scan", [r_ra, r_iu, r_hc[fo]], [r_t1],
                                         t_, a_, i_, hcarry[:, fo:fo + 1], ALU.mult, ALU.add)
                                    vcopy(hcarry[:, fo:fo + 1], t1[:, c2, n - 1:n], [r_t1], [r_hc[fo]])
                                else:
                                    tt(t_, a_, h0s[:, fo, :], ALU.mult, [r_ra, r_h0s], [r_t1])
                                    tt(t_, t_, i_, ALU.add, [r_t1, r_iu], [r_t1])
                                    vcopy(hs_keep[:, fo, :], t_, [r_t1], [r_hsk])
                                tt(y[:, fo, off:off + n], g_, t_, ALU.mult, [r_gel, r_t1], [r_y[bi]])
                        ws.release(("lx", tt_, j), ("lgb", tt_, j))
                    for j in range(2):
                        w, r_w = ws.next(("lo", tt_, j))
                        lin_fm(r_w, v3(8, 512)(w), 4, y, lambda bi: [r_y[bi]], blks,
                               lambda b, fo, off, n, bi, j=j: resid_add(b, 4 * j + fo, off, n, bi))
                        ws.release(("lo", tt_, j))

                phase()
                norm(xT, r_x, V_NXA + 8 * l, blks, hn, r_hn)
                qx = carve([8, TT], BF16)
                r_qx = [[ares() for _ in range(H)] for _ in range(3)]
                pf = [carve([MEM], F32) for _ in range(2)]
                pn = [carve([MEM], BF16) for _ in range(2)]
                pT = [carve([2, 512], BF16) for _ in range(2)]
                r_pf = [ares() for _ in range(2)]
                r_pn = [ares() for _ in range(2)]
                r_pT = [ares() for _ in range(2)]
                for j in range(2):
                    w, r_w = ws.next(("xq", tt_, l, j))
                    lin_fm(r_w, v3(8, 512)(w), 4, hn, lambda bi: [r_hn[bi]], blks,
                           lambda b, fo, off, n, bi, j=j: anycopy(qx[:, 4 * j + fo, off:off + n], bankf(b)[:, 0:n],
                                                                  [kb.pres[b]], [r_qx[bi][(4 * j + fo) // 2]]))
                    ws.release(("xq", tt_, l, j))
                acnt = 0
                for (off, n, bi) in pblks:
                    for h in range(H):
                        bT = kb.bank()
                        for c4 in range(4):
                            csl = slice(off + c4 * P, off + (c4 + 1) * P)
                            k2 = acnt % 2
                            acnt += 1
                            bs = kb.bank()
                            mm(bankf(bs)[:, 0:MEM], [(qx[:, 2 * h + half, csl], kTm[l][:, 2 * h + half, :]) for half in range(2)],
                               [r_qx[bi][h], r_kTm[l]], [kb.pres[bs]])
                            mx, r_mx = st(40 + 2 * k2)
                            den, r_den = st(41 + 2 * k2)
                            kb.I("dve", "reduce_max", [kb.pres[bs]], [r_mx], mx, bankf(bs)[:, 0:MEM], AX.X)
                            ts(mx, mx, -1.0 / 16.0, None, ALU.mult, None, [r_mx], [r_mx])
                            act(pf[k2], bankf(bs)[:, 0:MEM], AF.Exp, [kb.pres[bs], r_mx], [r_pf[k2], r_den],
                                scale=1.0 / 16.0, bias=mx, accum_out=den)
                            kb.I("dve", "reciprocal", [r_den], [r_den], den, den)
                            ts(pn[k2], pf[k2], den, None, ALU.mult, None, [r_pf[k2], r_den], [r_pn[k2]])
                            trs([(bankb(bT)[:, (mc * 4 + c4) * P:(mc * 4 + c4 + 1) * P], pn[k2][:, mc * P:(mc + 1) * P], idb[:])
                                 for mc in range(2)], [r_pn[k2], r_idb], [kb.pres[bT]])
                        pq = h % 2
                        anycopy(pT[pq], bankb(bT)[:, 0:1024].rearrange("p (a b) -> p a b", a=2), [kb.pres[bT]], [r_pT[pq]])
                        for e2 in range(2):
                            b = kb.bank()
                            mm(bankf(b), [(vtm[l][:, mc, h * 256 + e2 * P: h * 256 + (e2 + 1) * P], pT[pq][:, mc, :]) for mc in range(2)],
                               [r_vtm[l], r_pT[pq]], [kb.pres[b]])
                            anycopy(qx[:, 2 * h + e2, off:off + n], bankf(b), [kb.pres[b]], [r_qx[bi][h]])
                if last:
                    kin_all = PH.sq.rearrange("p a b -> p (a b)")
                    kin = [kin_all[:, i * 2048:(i + 1) * 2048].rearrange("p (a b) -> p a b", a=2) for i in range(2)]
                    kTs = [carve([8, MEM], BF16) for _ in range(2)]
                    vs = [carve([2, D], BF16) for _ in range(2)]
                    r_kin = [PH.r_sq, PH.r_sq]
                    r_kTs = [ares() for _ in range(2)]
                    r_vs = [ares() for _ in range(2)]
                    sTs = carve([2, 64], F32)
                    s_tm = carve([MEM], F32)
                    p_tm = carve([MEM], BF16)
                    pTs = carve([2, 64], BF16)
                    r_sTs, r_stm, r_ptm, r_pTs = (ares() for _ in range(4))
                    bS = kb.pin()
                    ts(bankf(bS)[:, 0:P], cident, 0.0, None, ALU.mult, None, [r_cst], [kb.pres[bS]])
                    for bb_ in range(NS):
                        k2 = bb_ % 2
                        for hf in range(2):
                            cs = slice(hf * 512, (hf + 1) * 512)
                            kb.dma("pool", kin[k2][:, :, cs], mem_k[l, bb_].rearrange("(k p) f -> p k f", p=P)[:, :, cs], [], [r_kin[k2]])
                            kb.dma("pool", vs[k2][:, :, cs], mem_v[l, bb_].rearrange("(k p) f -> p k f", p=P)[:, :, cs], [], [r_vs[k2]])
                        for f0 in (0, 4):
                            b = kb.bank()
                            items = []
                            for f in range(4):
                                for mc in range(2):
                                    items.append((bankb(b)[:, (f * 2 + mc) * P:(f * 2 + mc + 1) * P],
                                                  kin[k2][:, mc, (f0 + f) * P:(f0 + f + 1) * P], idb[:]))
                            trs(items, [r_kin[k2], r_idb], [kb.pres[b]])
                            anycopy(kTs[k2][:, f0:f0 + 4, :], bankb(b)[:, 0:1024].rearrange("p (a b) -> p a b", a=4),
                                    [kb.pres[b]], [r_kTs[k2]])
                        for h in range(H):
                            for mc in range(2):
                                col = mc * 64 + h * NS + bb_
                                mm(bankf(bS)[:, col:col + 1],
                                   [(kTs[k2][:, 2 * h + half, mc * P:(mc + 1) * P], qx[:, 2 * h + half, T + bb_:T + bb_ + 1])
                                    for half in range(2)], [r_kTs[k2], r_qx[2][h]], [kb.pres[bS]])
                        if k2 == 1:
                            vcopy(sTs, bankf(bS)[:, 0:P].rearrange("p (a b) -> p a b", a=2), [kb.pres[bS]], [r_sTs])
                            b = kb.bank()
                            trs([(bankf(b)[0:64, mc * P:(mc + 1) * P], sTs[:, mc, :], cident) for mc in range(2)],
                                [r_sTs, r_cst], [kb.pres[b]])
                            mx, r_mx = st(48, 64)
                            den, r_den = st(49, 64)
                            kb.I("dve", "reduce_max", [kb.pres[b]], [r_mx], mx, bankf(b)[0:64, 0:MEM], AX.X)
                            ts(mx, mx, -1.0 / 16.0, None, ALU.mult, None, [r_mx], [r_mx])
                            act(s_tm[0:64, :], bankf(b)[0:64, 0:MEM], AF.Exp, [kb.pres[b], r_mx], [r_stm, r_den],
                                scale=1.0 / 16.0, bias=mx, accum_out=den)
                            kb.I("dve", "reciprocal", [r_den], [r_den], den, den)
                            ts(p_tm[0:64, :], s_tm[0:64, :], den, None, ALU.mult, None, [r_stm, r_den], [r_ptm])
                            b2 = kb.bank()
                            trs([(bankb(b2)[:, mc * 64:(mc + 1) * 64], p_tm[0:64, mc * P:(mc + 1) * P], idb[0:64, 0:64]) for mc in range(2)],
                                [r_ptm, r_idb], [kb.pres[b2]])
                            anycopy(pTs, bankb(b2)[:, 0:P].rearrange("p (a b) -> p a b", a=2), [kb.pres[b2]], [r_pTs])
                            bO = kb.bank()
                            for sb_ in (bb_ - 1, bb_):
                                kk = sb_ % 2
                                for h in range(H):
                                    for e2 in range(2):
                                        col = (sb_ - (bb_ - 1)) * 8 + 2 * h + e2
                                        mm(bankf(bO)[:, col:col + 1],
                                           [(vs[kk][:, mc, h * 256 + e2 * P:h * 256 + (e2 + 1) * P],
                                             pTs[:, mc, h * NS + sb_:h * NS + sb_ + 1]) for mc in range(2)],
                                           [r_vs[kk], r_pTs], [kb.pres[bO]])
                            for s2 in range(2):
                                anycopy(qx[:, :, T + bb_ - 1 + s2:T + bb_ + s2], bankf(bO)[:, 8 * s2:8 * s2 + 8].rearrange("p (f o) -> p f o", o=1),
                                        [kb.pres[bO]], r_qx[2])
                    kb.unpin(bS)
                for j in range(2):
                    w, r_w = ws.next(("xo", tt_, l, j))
                    lin_fm(r_w, v3(8, 512)(w), 4, qx, lambda bi: r_qx[bi], blks,
                           lambda b, fo, off, n, bi, j=j: resid_add(b, 4 * j + fo, off, n, bi))
                    ws.release(("xo", tt_, l, j))

                phase()
                norm(xT, r_x, V_NFFN + 8 * l, blks, hn, r_hn)
                yh = carve([12, TT], BF16)
                r_yh = [ares() for _ in range(3)]
                ub0 = carve([4, 2 + 512], F32)
                ub = [ub0, ub0]
                uc = [PH.sq.rearrange("p a b -> p (a b)").bitcast(F32).rearrange("p (a b) -> p a b", a=4), carve([4, 512], F32)]
                r_ub0 = ares()
                r_ub = [r_ub0, r_ub0]
                r_uc = [PH.r_sq, ares()]
                fcnt = 0
                for hh in range(2):
                    for j3 in range(3):
                        j = 3 * hh + j3
                        wu, r_wu = ws.next(("fu", tt_, l, j))
                        wgt_, r_wgt_ = ws.next(("fg", tt_, l, j))
                        wuv = v3(8, 512)(wu)
                        wgv = v3(8, 512)(wgt_)
                        for (off, n, bi) in blks:
                            pb = fcnt % 2
                            fcnt += 1
                            smp = (bi == 2)
                            for f4 in range(4):
                                fo = 4 * j + f4
                                bu = kb.bank()
                                mm(bankf(bu)[:, 0:n], [(wuv[:, kc, f4 * P:(f4 + 1) * P], hn[:, kc, off:off + n]) for kc in range(8)],
                                   [r_wu, r_hn[bi]], [kb.pres[bu]])
                                bg = kb.bank()
                                mm(bankf(bg)[:, 0:n], [(wgv[:, kc, f4 * P:(f4 + 1) * P], hn[:, kc, off:off + n]) for kc in range(8)],
                                   [r_wgt_, r_hn[bi]], [kb.pres[bg]])
                                cw = lambda t_, fo=fo: vec[:, V_FCW + (l * 3 + t_) * 24 + fo: V_FCW + (l * 3 + t_) * 24 + fo + 1]
                                cbias = vec[:, V_FCB + l * 24 + fo:V_FCB + l * 24 + fo + 1]
                                uco = uc[pb][:, f4, 0:n]
                                if not smp:
                                    uh = ub[pb]
                                    acopy(uh[:, f4, 2:2 + n], bankf(bu)[:, 0:n], [kb.pres[bu]], [r_ub[pb]])
                                    vcopy(uh[:, f4, 0:2], ffn_halo[l][:, fo, :], [r_fhalo[l][fo]], [r_ub[pb]])
                                    vcopy(ffn_halo[l][:, fo, :], uh[:, f4, n:n + 2], [r_ub[pb]], [r_fhalo[l][fo]])
                                    act(uco, uh[:, f4, 2:2 + n], AF.Identity, [r_ub[pb], r_vec], [r_uc[pb]], scale=cw(2), bias=cbias)
                                    taps = [uh[:, f4, 1:1 + n], uh[:, f4, 0:n]]
                                    r_taps = r_ub[pb]
                                else:
                                    acopy(ukeep[l][:, fo, :], bankf(bu)[:, 0:n], [kb.pres[bu]], [r_uk[l]])
                                    act(uco, bankf(bu)[:, 0:n], AF.Identity, [kb.pres[bu], r_vec], [r_uc[pb]], scale=cw(2), bias=cbias)
                                    taps = [fcs[l][:, fo, :, 1], fcs[l][:, fo, :, 0]]
                                    r_taps = r_fcs[l]
                                for tap_ap, wcol in zip(taps, (cw(1), cw(0))):
                                    stt(uco, tap_ap, wcol, uco, ALU.mult, ALU.add, [r_taps, r_uc[pb], r_vec], [r_uc[pb]])
                                act(uco, uco, AF.Gelu_apprx_tanh, [r_uc[pb]], [r_uc[pb]])
                                tt(yh[:, 4 * j3 + f4, off:off + n], bankf(bg)[:, 0:n], uco, ALU.mult,
                                   [kb.pres[bg], r_uc[pb]], [r_yh[bi]])
                        ws.release(("fu", tt_, l, j), ("fg", tt_, l, j))
                    wd = []
                    for j3 in range(3):
                        j = 3 * hh + j3
                        w, r_w = ws.next(("fd", tt_, l, j))
                        wd.append((v3(4, 1024)(w), r_w))
                    for (off, n, bi) in blks:
                        for fc in range(8):
                            b = kb.bank()
                            mm(bankf(b)[:, 0:n],
                               [(wd[j3][0][:, f4, fc * P:(fc + 1) * P], yh[:, 4 * j3 + f4, off:off + n])
                                for j3 in range(3) for f4 in range(4)],
                               [wd[0][1], wd[1][1], wd[2][1], r_yh[bi]], [kb.pres[b]])
                            resid_add(b, fc, off, n, bi)
                    ws.release(*[("fd", tt_, l, 3 * hh + j3) for j3 in range(3)])

            phase(stage=True)
            fin = carve([8, 512], F32)
            r_fin = ares()
            for (off, n, bi) in blks:
                norm(xT, r_x, V_NFIN, [(off, n, bi)], _Shift(fin, off), [r_fin] * 3)
                if bi < 2:
                    for c4 in range(4):
                        fm_to_tm(lambda fc, c4=c4: fin[:, fc, c4 * P:(c4 + 1) * P], [r_fin], P, 8,
                                 y_p[tok0 + off + c4 * P: tok0 + off + (c4 + 1) * P, :])
                else:
                    fm_to_tm(lambda fc: fin[:, fc, 0:NS], [r_fin], NS, 8, y_s)

        phase(stage=True, sq_=False)
        b = kb.bank()
        trs([(bankf(b)[0:8, 0:P], hcarry[:], cident)], r_hc + [r_cst], [kb.pres[b]])
        t_h, r_th = stage_buf()
        acopy(t_h[0:8, 0:P], bankf(b)[0:8, 0:P], [kb.pres[b]], [r_th])
        out_ops.append(kb.dma("sp", o_lruh_p.rearrange("o (c q) -> (o c) q", q=P), t_h[0:8, 0:P], [r_th], []))
        b = kb.bank()
        trs([(bankf(b)[0:24, 0:P], lru_halo[:].rearrange("p a b -> p (a b)"), cident)], r_lhalo + [r_cst], [kb.pres[b]])
        t_c, r_tc2 = stage_buf()
        acopy(t_c[0:24, 0:P], bankf(b)[0:24, 0:P], [kb.pres[b]], [r_tc2])
        for fc in range(8):
            out_ops.append(kb.dma("sp", o_lrucv_p[:, fc * P:(fc + 1) * P], t_c[3 * fc:3 * fc + 3, 0:P], [r_tc2], []))
        for l in range(2):
            b = kb.bank()
            trs([(bankf(b)[0:48, 0:P], ffn_halo[l][:].rearrange("p a b -> p (a b)"), cident)], r_fhalo[l] + [r_cst], [kb.pres[b]])
            t_f, r_tf = stage_buf()
            acopy(t_f[0:48, 0:P], bankf(b)[0:48, 0:P], [kb.pres[b]], [r_tf])
            for fo in range(24):
                out_ops.append(kb.dma("sp", o_ffncv_p[l, :, fo * P:(fo + 1) * P], t_f[2 * fo:2 * fo + 2, 0:P], [r_tf], []))
        fm_to_tm(lambda fc: hs_keep[:, fc, :], [r_hsk], NS, 8, o_lruh_s)
        fm_to_tm(lambda fc: xbs_keep[:, fc, :], [r_xbs], NS, 8, o_lrucv_s[:, 2, :])
        out_ops.append(kb.dma("sp", o_lrucv_s[:, 0:2, :], lru_cv0[:, 1:3, :], [], []))
        for l in range(2):
            fm_to_tm(lambda fc, l=l: ukeep[l][:, fc, :], [r_uk[l]], NS, 24, o_ffncv_s[l, :, 1, :])
            out_ops.append(kb.dma("sp", o_ffncv_s[l, :, 0, :], ffn_cv0[l, :, 1, :], [], []))


    import os as _os
    try:
        record_all()
    except _StopBuild:
        pass

    fin_res = []
    lastd = {}
    for e in ENG:
        for o in kb.ops[e]:
            if o.dma:
                lastd[o.dsem] = o
    lastc = [kb.ops[e][-1] for e in ("pe", "act", "dve", "pool") if kb.ops[e] and not kb.ops[e][-1].dma]
    for o in list(out_ops) + list(lastd.values()) + lastc:
        r = Res()
        r.w = o
        fin_res.append(r)
    kb.op("sp", lambda e: e.nop(), rd=fin_res, wr=[])
    assert PH.stopped or ws.used == len(ws.pieces), (ws.used, len(ws.pieces))
    kb.emit(es)
    return nc, es


class _Shift:
    def __init__(self, base, off):
        self.base = base
        self.off = off

    def __getitem__(self, key):
        p, fc, sl = key
        return self.base[p, fc, sl.start - self.off: sl.stop - self.off]


def _fm(v):
    v = np.asarray(v, np.float32)
    return np.ascontiguousarray(v.reshape(-1, P).T)


def _consts():
    cst = np.zeros((P, NC), np.float32)
    cst[:, C_ID:C_ID + P] = np.eye(P, dtype=np.float32)
    j = np.arange(P, dtype=np.float64)
    for h in range(H):
        g = 1.0 - 2.0 ** (-5.0 - h)
        m = np.zeros((P, P), np.float64)
        m[:, :] = (g ** (-(j[:, None] + 1.0))) / 16.0
        m = np.where(j[None, :] >= j[:, None], m, 0.0)
        cst[:, C_MASK + h * P:C_MASK + (h + 1) * P] = m.astype(np.float32)
        cst[:, C_KDEC + h] = (g ** (127.0 - j) / 16.0).astype(np.float32)
        cst[:, C_EPSP + h] = (EPS / (g ** (2.0 * (j + 1.0)))).astype(np.float32)
    half = 128
    inv = (np.float32(10000.0) ** (-(np.arange(half, dtype=np.float32) / np.float32(half)))).astype(np.float32)
    rc = np.zeros((NT, P, TT), np.float32)
    rsn = np.zeros((NT, P, TT), np.float32)
    for t in range(NT):
        pos = np.concatenate([np.arange(t * T, (t + 1) * T), np.full(NS, 16384)]).astype(np.float32)
        ang = (pos[None, :] * inv[:, None]).astype(np.float32).astype(np.float64)
        rc[t] = np.cos(ang).astype(np.float32)
        rsn[t] = np.sin(ang).astype(np.float32)
    return cst, rc, rsn


_CACHE = {}


def kernel(x_prompt, x_sample, state_ret, state_lru_h, state_lru_conv, state_ffn_conv,
           cache_mem_k, cache_mem_v, mem_prompt, norm_mix, norm_xa, norm_mem, norm_ffn, norm_final,
           ret_w_in, ret_w_out, lru_w_in, lru_conv_w, lru_conv_b, lru_wa, lru_ba, lru_wx, lru_bx,
           lru_lambda, lru_w_out, xa_w_q, xa_w_kv, xa_w_o, ffn_w_up, ffn_conv_w, ffn_conv_b, ffn_w_down):
    f = lambda a: np.ascontiguousarray(np.asarray(a, dtype=np.float32))
    n = 8
    vec = np.zeros((P, NV), np.float32)
    for l in range(2):
        vec[:, V_NMIX + 8 * l:V_NMIX + 8 * l + 8] = _fm(norm_mix[l])
        vec[:, V_NXA + 8 * l:V_NXA + 8 * l + 8] = _fm(norm_xa[l])
        vec[:, V_NMEM + 8 * l:V_NMEM + 8 * l + 8] = _fm(norm_mem[l])
        vec[:, V_NFFN + 8 * l:V_NFFN + 8 * l + 8] = _fm(norm_ffn[l])
        for t in range(3):
            vec[:, V_FCW + (l * 3 + t) * 24:V_FCW + (l * 3 + t + 1) * 24] = _fm(np.asarray(ffn_conv_w)[l, t])
        vec[:, V_FCB + l * 24:V_FCB + (l + 1) * 24] = _fm(np.asarray(ffn_conv_b)[l])
    vec[:, V_NFIN:V_NFIN + 8] = _fm(norm_final)
    for t in range(4):
        vec[:, V_LCW + 8 * t:V_LCW + 8 * t + 8] = _fm(np.asarray(lru_conv_w)[0, t])
    vec[:, V_LCB:V_LCB + 8] = _fm(np.asarray(lru_conv_b)[0])
    vec[:, V_LBA:V_LBA + 8] = _fm(np.asarray(lru_ba)[0].reshape(-1))
    vec[:, V_LBX:V_LBX + 8] = _fm(np.asarray(lru_bx)[0].reshape(-1))
    vec[:, V_LLAM:V_LLAM + 8] = _fm(np.asarray(lru_lambda)[0])
    cst, rc, rsn = _consts()
    shared = {
        "vec": vec, "cst": cst, "rope_c": rc, "rope_s": rsn,
        "ret_w_in": f(ret_w_in)[0], "ret_w_out": f(ret_w_out)[0], "lru_w_in": f(lru_w_in)[0],
        "lru_wa": f(lru_wa)[0], "lru_wx": f(lru_wx)[0], "lru_w_out": f(lru_w_out)[0],
        "xa_w_q": f(xa_w_q), "xa_w_kv": f(xa_w_kv), "xa_w_o": f(xa_w_o),
        "ffn_w_up": f(ffn_w_up), "ffn_w_down": f(ffn_w_down),
    }
    x_prompt = f(x_prompt); x_sample = f(x_sample); state_ret = f(state_ret); state_lru_h = f(state_lru_h)
    state_lru_conv = f(state_lru_conv); state_ffn_conv = f(state_ffn_conv)
    cache_mem_k = f(cache_mem_k); cache_mem_v = f(cache_mem_v); mem_prompt = f(mem_prompt)
    in_maps = []
    for c in range(n):
        s = slice(NS * c, NS * (c + 1))
        m = dict(shared)
        m.update({
            "x_p": x_prompt[c], "x_s": x_sample[s, 0, :], "ret_s0": state_ret[0, s],
            "lru_h0": state_lru_h[0, s], "lru_cv0": state_lru_conv[0, s],
            "ffn_cv0": np.ascontiguousarray(state_ffn_conv[:, s]),
            "mem_k": np.ascontiguousarray(cache_mem_k[:, s].reshape(2, NS, MEM, D)),
            "mem_v": np.ascontiguousarray(cache_mem_v[:, s].reshape(2, NS, MEM, D)),
            "mem_p": mem_prompt[c],
        })
        in_maps.append(m)
    if "nc" not in _CACHE:
        _CACHE["nc"] = build_program()
    nc, _es = _CACHE["nc"]
    res = run_bass_kernel_spmd(nc, in_maps, core_ids=list(range(n)))
    R_ = res.results
    cat = lambda k: np.concatenate([np.asarray(r[k]) for r in R_], axis=0)
    stk = lambda k: np.stack([np.asarray(r[k]) for r in R_], axis=0)
    y_prompt = stk("y_p")
    y_sample = cat("y_s").reshape(n * NS, 1, D)
    new_ret_p = stk("o_ret_p")[None]
    new_ret_s = cat("o_ret_s")[None]
    new_lru_h_p = cat("o_lruh_p")[None]
    new_lru_h_s = cat("o_lruh_s")[None]
    new_lru_conv_p = stk("o_lrucv_p")[None]
    new_lru_conv_s = cat("o_lrucv_s")[None]
    new_ffn_conv_p = np.stack([np.asarray(r["o_ffncv_p"]) for r in R_], axis=1)
    new_ffn_conv_s = np.concatenate([np.asarray(r["o_ffncv_s"]) for r in R_], axis=1)
    new_mem_k_p = np.stack([np.asarray(r["o_memk_p"]) for r in R_], axis=1).reshape(2, n, MEM, 4, 256)
    new_mem_v_p = np.stack([np.asarray(r["o_memv_p"]) for r in R_], axis=1).reshape(2, n, MEM, 4, 256)
    outs = (y_prompt, y_sample, new_ret_p, new_ret_s, new_lru_h_p, new_lru_h_s, new_lru_conv_p,
            new_lru_conv_s, new_ffn_conv_p, new_ffn_conv_s, new_mem_k_p, new_mem_v_p)
    return tuple(np.ascontiguousarray(o.astype(np.float32)) for o in outs)
```

```python
import contextlib
import numpy as np
import concourse.bass as bass
import concourse.mybir as mybir
from concourse.bass_utils import run_bass_kernel_spmd

F32 = mybir.dt.float32
BF16 = mybir.dt.bfloat16
F32R = mybir.dt.float32r
AF = mybir.ActivationFunctionType
ALU = mybir.AluOpType
AX = mybir.AxisListType

P = 128
D = 1024
SEQ = 2048
T = 1024
NT = SEQ // T
NS = 16
TT = T + NS
H = 4
FFN = 3072
MEM = 256
EPS = 1e-6
NSLOT = 5
ENG = ("pe", "act", "dve", "pool", "sp")

V_NMIX, V_NXA, V_NMEM, V_NFFN, V_NFIN = 0, 16, 32, 48, 64
V_LCW, V_LCB, V_LBA, V_LBX, V_LLAM = 72, 104, 112, 120, 128
V_FCW, V_FCB = 136, 280
NV = 328
C_ID, C_MASK, C_KDEC, C_EPSP = 0, 128, 640, 644
NC = 648


import os
_STOP_AFTER = int(os.environ.get("KSTOP", "100000"))
_KSUB = int(os.environ.get("KSUB", "0"))


class _StopBuild(Exception):
    pass


class Res:
    __slots__ = ("w", "r")

    def __init__(self, inherit=()):
        self.w = None
        self.r = list(inherit)


class Op:
    __slots__ = ("fn", "deps", "eng", "idx", "dma", "ms", "msn", "dsem", "dtgt")


class KB:
    ND = int(os.environ.get("KND", "8"))

    def __init__(self, nc):
        self.nc = nc
        self.ops = {e: [] for e in ENG}
        self.dcnt = {e: 0 for e in ENG}
        self.inherit = []
        self.phase_res = []
        self.pres = [Res() for _ in range(8)]
        self.pcur = 0
        self.pinned = set()

    def res(self, arena=False):
        r = Res(self.inherit if arena else ())
        if arena:
            self.phase_res.append(r)
        return r

    def begin_phase(self):
        toks = {}

        def merge(t):
            if t.dma:
                k = ("d",) + t.dsem
                if k not in toks or toks[k].dtgt < t.dtgt:
                    toks[k] = t
            else:
                k = t.eng
                if k not in toks or toks[k].idx < t.idx:
                    toks[k] = t

        for t in self.inherit:
            merge(t)
        for r in self.phase_res:
            if r.w is not None:
                merge(r.w)
            for t in r.r:
                merge(t)
        self.inherit = list(toks.values())
        self.phase_res = []

    def bank(self):
        while self.pcur in self.pinned:
            self.pcur = (self.pcur + 1) % 8
        i = self.pcur
        self.pcur = (self.pcur + 1) % 8
        return i

    def pin(self):
        i = self.bank()
        self.pinned.add(i)
        return i

    def unpin(self, i):
        self.pinned.discard(i)

    def op(self, eng, fn, rd=(), wr=(), dma=False):
        o = Op()
        o.fn = fn
        o.eng = eng
        o.idx = len(self.ops[eng])
        o.dma = dma
        o.ms = False
        o.msn = 0
        deps = {}

        def add(t, war=False):
            if t is None:
                return
            if (not t.dma) and t.eng == eng and (not dma):
                if eng == "pe":
                    return
            deps[id(t)] = t

        for r in rd:
            add(r.w)
        for r in wr:
            add(r.w)
            for t in r.r:
                add(t, war=True)
        o.deps = list(deps.values())
        for t in o.deps:
            if not t.dma:
                t.ms = True
        if dma:
            j = self.dcnt[eng]
            self.dcnt[eng] += 1
            o.dsem = (eng, j % self.ND)
            o.dtgt = 16 * (j // self.ND + 1)
        self.ops[eng].append(o)
        for r in rd:
            r.r.append(o)
        for r in wr:
            r.w = o
            r.r = []
        return o

    def I(self, eng, meth, rd, wr, *a, **k):
        return self.op(eng, lambda e: getattr(e, meth)(*a, **k), rd=rd, wr=wr)

    def dma(self, q, out, in_, rd, wr):
        return self.op(q, lambda e: e.dma_start(out=out, in_=in_), rd=rd, wr=wr, dma=True)

    def emit(self, es):
        nc = self.nc
        sem = {e: es.enter_context(nc.semaphore("s_" + e)) for e in ENG}
        dsem = {}
        for e in ("sp", "pool", "act"):
            for i in range(self.ND):
                dsem[(e, i)] = es.enter_context(nc.semaphore("d_%s%d" % (e, i)))
        for e in ENG:
            c = 0
            for o in self.ops[e]:
                if o.ms and not o.dma:
                    c += 1
                    o.msn = c
        ops = self.ops

        def run(e):
            def body(eng):
                waited = {}

                def wait(key, sm, val):
                    if waited.get(key, 0) < val:
                        eng.wait_ge(sm, val)
                        waited[key] = val

                for o in ops[e]:
                    for t in o.deps:
                        if t.dma:
                            wait(t.dsem, dsem[t.dsem], t.dtgt)
                        else:
                            wait(t.eng, sem[t.eng], t.msn)
                    if o.dma:
                        if o.dtgt > 16:
                            wait(o.dsem, dsem[o.dsem], o.dtgt - 16)
                        ins = o.fn(eng)
                        ins.then_inc(dsem[o.dsem], 16)
                    else:
                        ins = o.fn(eng)
                        if o.ms:
                            ins.then_inc(sem[e], 1)

            return body

        with nc.Block() as block:
            block.tensor(run("pe"))
            block.scalar(run("act"))
            block.vector(run("dve"))
            block.gpsimd(run("pool"))
            block.sync(run("sp"))


def build_program(dbg=None):
    nc = bass.Bass("TRN2", target_bir_lowering=False)
    es = contextlib.ExitStack()
    kb = KB(nc)

    def din(name, shape):
        return nc.dram_tensor(name, list(shape), F32, kind="ExternalInput").ap()

    def dout(name, shape):
        return nc.dram_tensor(name, list(shape), F32, kind="ExternalOutput").ap()

    x_p = din("x_p", [SEQ, D])
    x_s = din("x_s", [NS, D])
    ret_s0 = din("ret_s0", [NS, H, 256, 512])
    lru_h0 = din("lru_h0", [NS, D])
    lru_cv0 = din("lru_cv0", [NS, 3, D])
    ffn_cv0 = din("ffn_cv0", [2, NS, 2, FFN])
    mem_k = din("mem_k", [2, NS, MEM, D])
    mem_v = din("mem_v", [2, NS, MEM, D])
    mem_p = din("mem_p", [MEM, D])
    vec_d = din("vec", [P, NV])
    cst_d = din("cst", [P, NC])
    rope_c = din("rope_c", [NT, P, TT])
    rope_s = din("rope_s", [NT, P, TT])
    ret_w_in = din("ret_w_in", [D, 6144])
    ret_w_out = din("ret_w_out", [2048, D])
    lru_w_in = din("lru_w_in", [D, 2048])
    lru_wa = din("lru_wa", [4, 256, 256])
    lru_wx = din("lru_wx", [4, 256, 256])
    lru_w_out = din("lru_w_out", [D, D])
    xa_w_q = din("xa_w_q", [2, D, D])
    xa_w_kv = din("xa_w_kv", [2, D, 2048])
    xa_w_o = din("xa_w_o", [2, D, D])
    ffn_w_up = din("ffn_w_up", [2, D, 6144])
    ffn_w_down = din("ffn_w_down", [2, FFN, D])

    y_p = dout("y_p", [SEQ, D])
    y_s = dout("y_s", [NS, D])
    o_ret_p = dout("o_ret_p", [H, 256, 512])
    o_ret_s = dout("o_ret_s", [NS, H, 256, 512])
    o_lruh_p = dout("o_lruh_p", [1, D])
    o_lruh_s = dout("o_lruh_s", [NS, D])
    o_lrucv_p = dout("o_lrucv_p", [3, D])
    o_lrucv_s = dout("o_lrucv_s", [NS, 3, D])
    o_ffncv_p = dout("o_ffncv_p", [2, 2, FFN])
    o_ffncv_s = dout("o_ffncv_s", [2, NS, 2, FFN])
    o_memk_p = dout("o_memk_p", [2, MEM, D])
    o_memv_p = dout("o_memv_p", [2, MEM, D])
    out_ops = []

    def sb(name, shape, dt):
        return es.enter_context(nc.sbuf_tensor(name, list(shape), dt))

    ps = [es.enter_context(nc.psum_tensor("ps%d" % i, [P, 512], F32)) for i in range(8)]
    cst = sb("cst_sb", [P, NC], F32)
    vec = sb("vecs", [P, NV], F32)
    idb = sb("idb", [P, P], BF16)
    onesb = sb("onesb", [P, P], BF16)
    c8 = sb("c8", [P, 8], F32)
    xT = sb("xT", [P, 8, TT], F32)
    hn = sb("hn", [P, 8, TT], BF16)
    wsl = [sb("w%d" % i, [P, 4096], BF16) for i in range(NSLOT)]
    S = sb("S", [P, H, 2, 512], F32)
    S_r = [sb("S_r%d" % i, [P, 2, 512], BF16) for i in range(2)]
    kTm = [sb("kTm%d" % l, [P, 8, MEM], BF16) for l in range(2)]
    vtm = [sb("vtm%d" % l, [P, 2, D], BF16) for l in range(2)]
    rstd = [sb("rstd%d" % i, [P, 512], F32) for i in range(2)]
    lru_halo = sb("lru_halo", [P, 8, 3], F32)
    hcarry = sb("hcarry", [P, 8], F32)
    ffn_halo = [sb("ffn_halo%d" % l, [P, 24, 2], F32) for l in range(2)]
    h0s = sb("h0s", [P, 8, NS], F32)
    cvs = sb("cvs", [P, 8, NS, 3], F32)
    fcs = [sb("fcs%d" % l, [P, 24, NS, 2], F32) for l in range(2)]
    xbs_keep = sb("xbs_keep", [P, 8, NS], F32)
    hs_keep = sb("hs_keep", [P, 8, NS], F32)
    ukeep = [sb("ukeep%d" % l, [P, 24, NS], F32) for l in range(2)]
    stat = sb("stat", [P, 64], F32)
    ARENA_W = 13900
    S0b = sb("S0b", [P, 2, 512], F32R)
    qmr = sb("qmr", [P, 2, NS, NS], F32R)
    arena = sb("arena", [P, ARENA_W], F32)

    R = {}

    def rs(key, arena_=False):
        if key not in R:
            R[key] = kb.res(arena_)
        return R[key]

    r_cst, r_vec, r_idb, r_ones, r_c8 = (kb.res() for _ in range(5))
    r_x = [kb.res() for _ in range(3)]
    r_hn = [kb.res() for _ in range(3)]
    r_ws = [kb.res() for _ in range(NSLOT)]
    r_S = [kb.res() for _ in range(H)]
    r_Sr = [kb.res() for _ in range(2)]
    r_kTm = [kb.res() for _ in range(2)]
    r_vtm = [kb.res() for _ in range(2)]
    r_rstd = [kb.res() for _ in range(2)]
    r_lhalo = [kb.res() for _ in range(8)]
    r_hc = [kb.res() for _ in range(8)]
    r_fhalo = [[kb.res() for _ in range(24)] for _ in range(2)]
    r_h0s, r_cvs, r_xbs, r_hsk = (kb.res() for _ in range(4))
    r_fcs = [kb.res() for _ in range(2)]
    r_uk = [kb.res() for _ in range(2)]
    cident = cst[:, C_ID:C_ID + P]

    class Ar:
        off = 0

    def phase():
        kb.begin_phase()
        Ar.off = 0

    def carve(shape, dt):
        n = int(np.prod(shape))
        words = n if dt == F32 else (n + 1) // 2
        words = (words + 7) // 8 * 8
        a = arena[:, Ar.off:Ar.off + words]
        Ar.off += words
        assert Ar.off <= ARENA_W, Ar.off
        if dt != F32:
            a = a.bitcast(dt)[:, 0:n]
        else:
            a = a[:, 0:n]
        if len(shape) == 2:
            a = a.rearrange("p (a b) -> p a b", a=shape[0])
        elif len(shape) == 3:
            a = a.rearrange("p (a b c) -> p a b c", a=shape[0], b=shape[1])
        return a

    def ares():
        return kb.res(True)

    def bankf(i):
        return ps[i][:]

    def bankb(i):
        return ps[i][:].bitcast(BF16)

    def mm(out, pairs, rd, wr, start=True, stop=True):
        def fn(e):
            n = len(pairs)
            ins = None
            for i, (l, r) in enumerate(pairs):
                ins = e.matmul(out, l, r, start=(start and i == 0), stop=(stop and i == n - 1))
            return ins
        return kb.op("pe", fn, rd=rd, wr=wr)

    def trs(items, rd, wr):
        def fn(e):
            ins = None
            for (o, i, idn) in items:
                ins = e.transpose(o, i, idn)
            return ins
        return kb.op("pe", fn, rd=rd, wr=wr)

    def act(out, in_, func, rd, wr, **k):
        return kb.I("act", "activation", rd, wr, out, in_, func, **k)

    def tt(out, a, b, op, rd, wr, eng="dve"):
        return kb.I(eng, "tensor_tensor", rd, wr, out, a, b, op)

    def ts(out, a, s1, s2, op0, op1, rd, wr, **k):
        if op1 is None:
            return kb.I("dve", "tensor_scalar", rd, wr, out, a, s1, None, op0, **k)
        return kb.I("dve", "tensor_scalar", rd, wr, out, a, s1, s2, op0, op1, **k)

    def stt(out, a, s, b, op0, op1, rd, wr):
        return kb.I("dve", "scalar_tensor_tensor", rd, wr, out, a, s, b, op0, op1)

    def vcopy(out, in_, rd, wr):
        return kb.I("dve", "tensor_copy", rd, wr, out, in_)

    def acopy(out, in_, rd, wr, **k):
        return act(out, in_, AF.Copy, rd, wr, **k)

    cp_flip = [0]

    def anycopy(out, in_, rd, wr):
        cp_flip[0] ^= 1
        if cp_flip[0]:
            return acopy(out, in_, rd, wr)
        return vcopy(out, in_, rd, wr)

    def rsqrt_inplace(ap, rd_extra, res, scale, bias):
        act(ap, ap, AF.Sqrt, rd_extra + [res], [res], scale=scale, bias=bias)
        kb.I("dve", "reciprocal", [res], [res], ap, ap)

    class WS:
        def __init__(self):
            self.pieces = []
            self.issued = 0
            self.used = 0
            self.free = list(range(NSLOT))
            self.slot = {}

        def add(self, key, dmas):
            self.pieces.append((key, dmas))

        def _prefetch(self):
            while self.free and self.issued < len(self.pieces):
                key, dmas = self.pieces[self.issued]
                s = self.free.pop(0)
                self.slot[key] = s
                for (dstf, src) in dmas:
                    kb.dma("pool", dstf(wsl[s]), src, [], [r_ws[s]])
                self.issued += 1

        def next(self, key):
            i = self.used
            assert self.pieces[i][0] == key, (self.pieces[i][0], key)
            self._prefetch()
            assert key in self.slot, ("no free weight slot for", key)
            self.used += 1
            s = self.slot[key]
            return wsl[s], r_ws[s]

        def release(self, *keys):
            for key in keys:
                self.free.append(self.slot.pop(key))
            self._prefetch()

    ws = WS()

    def v3(a, b):
        return lambda t: t[:, 0:a * b].rearrange("p (a b) -> p a b", a=a)

    def v3h(hf):
        return lambda t: t[:, 0:4096].rearrange("p (a b) -> p a b", a=4)[:, :, hf * 512:(hf + 1) * 512]

    def wcols(w, c0, nc_):
        return w[:, c0:c0 + nc_].rearrange("(k p) f -> p k f", p=P)

    def wrows(w, r0):
        return w[r0:r0 + 512, :].rearrange("(k p) f -> p k f", p=P)

    for l in range(2):
        for j in range(4):
            ws.add(("kv", l, j), [(v3(8, 512), wcols(xa_w_kv[l], 512 * j, 512))])
    for tt_ in range(NT):
        for h in range(H):
            ws.add(("rq", tt_, h), [(v3(8, 256), wcols(ret_w_in, 256 * h, 256))])
            ws.add(("rk", tt_, h), [(v3(8, 256), wcols(ret_w_in, 1024 + 256 * h, 256))])
            ws.add(("rv", tt_, h), [(v3(8, 512), wcols(ret_w_in, 2048 + 512 * h, 512))])
            ws.add(("rg", tt_, h), [(v3(8, 512), wcols(ret_w_in, 4096 + 512 * h, 512))])
            ws.add(("ro", tt_, h), [(v3h(0), wrows(ret_w_out, 512 * h)[:, :, 0:512]),
                                    (v3h(1), wrows(ret_w_out, 512 * h)[:, :, 512:1024])])
        for l in range(2):
            if l == 1:
                for j in range(4):
                    ws.add(("lx", tt_, j), [(v3(8, 256), wcols(lru_w_in, 256 * j, 256))])
                    ws.add(("lgb", tt_, j), [(v3(8, 256), wcols(lru_w_in, 1024 + 256 * j, 256))])
                for j in range(2):
                    ws.add(("lo", tt_, j), [(v3(8, 512), wcols(lru_w_out, 512 * j, 512))])
            for j in range(2):
                ws.add(("xq", tt_, l, j), [(v3(8, 512), wcols(xa_w_q[l], 512 * j, 512))])
            for j in range(2):
                ws.add(("xo", tt_, l, j), [(v3(8, 512), wcols(xa_w_o[l], 512 * j, 512))])
            for hh in range(2):
                for j3 in range(3):
                    j = 3 * hh + j3
                    ws.add(("fu", tt_, l, j), [(v3(8, 512), wcols(ffn_w_up[l], 512 * j, 512))])
                    ws.add(("fg", tt_, l, j), [(v3(8, 512), wcols(ffn_w_up[l], FFN + 512 * j, 512))])
                for j3 in range(3):
                    j = 3 * hh + j3
                    ws.add(("fd", tt_, l, j), [(v3h(0), wrows(ffn_w_down[l], 512 * j)[:, :, 0:512]),
                                               (v3h(1), wrows(ffn_w_down[l], 512 * j)[:, :, 512:1024])])

    def blocks_of(tt_):
        b = [(0, 512, 0), (512, 512, 1)]
        if tt_ == NT - 1:
            b.append((T, NS, 2))
        return b

    statres = {}

    def st(col, n=P):
        if col not in statres:
            statres[col] = kb.res()
        return stat[0:n, col:col + 1], statres[col]

    class PH:
        sq = None
        r_sq = None
        stg = None
        r_stg = None
        k = 0
        count = 0
        stopped = False

    def phase(stage=False, sq_=True):
        PH.count += 1
        if PH.count > _STOP_AFTER:
            PH.stopped = True
            raise _StopBuild()
        kb.begin_phase()
        Ar.off = 0
        PH.sq = carve([8, 512], BF16) if sq_ else None
        PH.r_sq = ares() if sq_ else None
        if stage:
            PH.stg = [carve([512], F32) for _ in range(2)]
            PH.r_stg = [ares() for _ in range(2)]
        PH.k = 0

    def stage_buf():
        i = PH.k % 2
        PH.k += 1
        return PH.stg[i], PH.r_stg[i]

    def tm_to_fm(src_dram, ntok, nfc, dstf, r_dst):
        for f0 in range(0, nfc, 4):
            nf = min(4, nfc - f0)
            tin, r_in = stage_buf()
            kb.dma("sp", tin[0:ntok, 0:nf * P], src_dram[:, f0 * P:(f0 + nf) * P], [], [r_in])
            b = kb.bank()
            items = [(bankf(b)[:, k * ntok:(k + 1) * ntok], tin[0:ntok, k * P:(k + 1) * P],
                      cident[0:ntok, 0:ntok]) for k in range(nf)]
            trs(items, [r_in, r_cst], [kb.pres[b]])
            anycopy(dstf(f0, nf), bankf(b)[:, 0:nf * ntok].rearrange("p (a b) -> p a b", a=nf),
                    [kb.pres[b]], [r_dst])

    def fm_to_tm(srcf, r_src, ntok, nfc, dst_dram, q="sp"):
        for f0 in range(0, nfc, 4):
            nf = min(4, nfc - f0)
            b = kb.bank()
            items = [(bankf(b)[0:ntok, k * P:(k + 1) * P], srcf(f0 + k), cident) for k in range(nf)]
            trs(items, r_src + [r_cst], [kb.pres[b]])
            tout, r_o = stage_buf()
            anycopy(tout[0:ntok, 0:nf * P], bankf(b)[0:ntok, 0:nf * P], [kb.pres[b]], [r_o])
            out_ops.append(kb.dma(q, dst_dram[:, f0 * P:(f0 + nf) * P], tout[0:ntok, 0:nf * P], [r_o], []))

    def norm(src, r_src_b, gcol, blks, dst, r_dst_b):
        sq, r_sq = PH.sq, PH.r_sq
        for (off, n, bi) in blks:
            act(sq[:, :, 0:n], src[:, :, off:off + n], AF.Square, [r_src_b[bi]], [r_sq])
            b = kb.bank()
            mm(bankf(b)[:, 0:n], [(onesb[:], sq[:, fc, 0:n]) for fc in range(8)], [r_sq, r_ones], [kb.pres[b]])
            rb = bi % 2
            act(rstd[rb][:, 0:n], bankf(b)[:, 0:n], AF.Sqrt, [kb.pres[b]], [r_rstd[rb]], scale=1.0 / D, bias=EPS)
            kb.I("dve", "reciprocal", [r_rstd[rb]], [r_rstd[rb]], rstd[rb][:, 0:n], rstd[rb][:, 0:n])
            for fc in range(8):
                stt(dst[:, fc, off:off + n], src[:, fc, off:off + n], vec[:, gcol + fc:gcol + fc + 1],
                    rstd[rb][:, 0:n], ALU.mult, ALU.mult, [r_src_b[bi], r_rstd[rb], r_vec], [r_dst_b[bi]])

    def lin_fm(r_w, wview, nfo, src, r_src_f, blks, cb, kc_n=8):
        for (off, n, bi) in blks:
            for fo in range(nfo):
                b = kb.bank()
                mm(bankf(b)[:, 0:n], [(wview[:, kc, fo * P:(fo + 1) * P], src[:, kc, off:off + n]) for kc in range(kc_n)],
                   [r_w] + r_src_f(bi), [kb.pres[b]])
                cb(b, fo, off, n, bi)

    def resid_add(b, fc, off, n, bi):
        tt(xT[:, fc, off:off + n], bankf(b)[:, 0:n], xT[:, fc, off:off + n], ALU.add, [kb.pres[b], r_x[bi]], [r_x[bi]])

    kb.dma("sp", cst[:], cst_d, [], [r_cst])
    kb.dma("sp", vec[:], vec_d, [], [r_vec])
    acopy(idb[:], cident, [r_cst], [r_idb])
    kb.I("dve", "memset", [], [r_ones], onesb[:], 1.0)
    act(c8[:], vec[:, V_LLAM:V_LLAM + 8], AF.Sigmoid, [r_vec], [r_c8])
    act(c8[:], c8[:], AF.Ln, [r_c8], [r_c8])
    ts(c8[:], c8[:], 8.0, None, ALU.mult, None, [r_c8], [r_c8])
    for h in range(H):
        kb.I("dve", "memset", [], [r_S[h]], S[:, h], 0.0)
    for fc in range(8):
        kb.I("dve", "memset", [], [r_lhalo[fc]], lru_halo[:, fc, :], 0.0)
        kb.I("dve", "memset", [], [r_hc[fc]], hcarry[:, fc:fc + 1], 0.0)
    for l in range(2):
        kb.I("dve", "memset", [], r_fhalo[l], ffn_halo[l][:], 0.0)

    def record_all():
        phase(stage=True)
        memT = carve([8, MEM], F32)
        mhn = carve([8, MEM], BF16)
        r_memT, r_mhn = ares(), ares()
        for mc in range(2):
            tm_to_fm(mem_p[mc * P:(mc + 1) * P, :], P, 8,
                     lambda f0, n, mc=mc: memT[:, f0:f0 + n, mc * P:(mc + 1) * P], r_memT)
        def sub(n):
            if _KSUB == n:
                PH.stopped = True
                raise _StopBuild()
        sub(1)
        for l in range(2):
            norm(memT, [r_memT], V_NMEM + 8 * l, [(0, MEM, 0)], mhn, [r_mhn])
            sub(2)
            for j in range(4):
                w, r_w = ws.next(("kv", l, j))
                sub(3)
                wv = v3(8, 512)(w)
                if j < 2:
                    def cbk(b, fo, off, n, bi, j=j, l=l):
                        anycopy(kTm[l][:, 4 * j + fo, :], bankf(b)[:, 0:MEM], [kb.pres[b]], [r_kTm[l]])
                    lin_fm(r_w, wv, 4, mhn, lambda bi: [r_mhn], [(0, MEM, 0)], cbk)
                sub(4)
                for mc in range(2):
                    b = kb.bank()
                    mm(bankf(b), [(mhn[:, kc, mc * P:(mc + 1) * P], wv[:, kc, :]) for kc in range(8)],
                       [r_w, r_mhn], [kb.pres[b]])
                    tout, r_o = stage_buf()
                    acopy(tout, bankf(b), [kb.pres[b]], [r_o])
                    sub(5)
                    dst = (o_memk_p if j < 2 else o_memv_p)[l, mc * P:(mc + 1) * P, (j % 2) * 512:(j % 2 + 1) * 512]
                    out_ops.append(kb.dma("sp", dst, tout, [r_o], []))
                    sub(6)
                    if j >= 2:
                        vcopy(vtm[l][:, mc, (j - 2) * 512:(j - 1) * 512], tout, [r_o], [r_vtm[l]])
                sub(100 + 10 * l + j)
                ws.release(("kv", l, j))
                sub(200 + 10 * l + j)

        for tt_ in range(NT):
            blks = blocks_of(tt_)
            pblks = blks[:2]
            last = (tt_ == NT - 1)
            tok0 = tt_ * T

            phase(stage=True, sq_=False)
            for ci in range(8):
                tm_to_fm(x_p[tok0 + ci * P: tok0 + (ci + 1) * P, :], P, 8,
                         lambda f0, n, ci=ci: xT[:, f0:f0 + n, ci * P:(ci + 1) * P], r_x[ci // 4])
            if last:
                tm_to_fm(x_s, NS, 8, lambda f0, n: xT[:, f0:f0 + n, T:TT], r_x[2])
                tm_to_fm(lru_h0, NS, 8, lambda f0, n: h0s[:, f0:f0 + n, :], r_h0s)
                tm_to_fm(lru_cv0.rearrange("b t f -> (b t) f"), NS * 3, 8,
                         lambda f0, n: cvs[:, f0:f0 + n].rearrange("p a b t -> p a (b t)"), r_cvs)
                for l in range(2):
                    tm_to_fm(ffn_cv0[l].rearrange("b t f -> (b t) f"), NS * 2, 24,
                             lambda f0, n, l=l: fcs[l][:, f0:f0 + n].rearrange("p a b t -> p a (b t)"), r_fcs[l])

            for l in range(2):
                phase()
                norm(xT, r_x, V_NMIX + 8 * l, blks, hn, r_hn)
                if l == 0:
                    tmpb = PH.sq.rearrange("p a b -> p (a b)").bitcast(F32)
                    tmp = [tmpb[:, i * 512:(i + 1) * 512] for i in range(4)]
                    r_tmp = PH.r_sq
                    ropec = carve([512], F32)
                    ropes = carve([512], F32)
                    r_rope = ares()
                    qT = carve([2, 512], BF16)
                    kT = carve([2, 512], BF16)
                    k_tm = carve([4, 256], BF16)
                    v_tm = carve([4, 512], BF16)
                    sg = carve([4, 512], BF16)
                    ogT = carve([4, 512], BF16)
                    attT = [carve([P], BF16) for _ in range(2)]
                    og = [carve([512], BF16) for _ in range(2)]
                    junk = carve([512], BF16)
                    r_qT, r_kT, r_ktm, r_vt, r_sg, r_ogT, r_junk = (ares() for _ in range(7))
                    r_att = [ares() for _ in range(2)]
                    r_og = [ares() for _ in range(2)]
                    if last:
                        Snew = [carve([512], F32) for _ in range(2)]
                        kmask = [carve([256], BF16) for _ in range(2)]
                        prod = carve([2, NS], BF16)
                        tcross = carve([512], F32)
                        o_s = carve([512], F32)
                        ogs = carve([512], BF16)
                        r_S0 = ares()
                        r_Sn = [ares() for _ in range(2)]
                        r_qm, r_prod, r_tc, r_os, r_ogs = (ares() for _ in range(5))
                        r_km = [ares() for _ in range(2)]
                    cnt2 = [0]
                    for h in range(H):
                        g_h = 1.0 - 2.0 ** (-5.0 - h)
                        cdec = float(np.float64(g_h) ** 128)
                        wq, r_wq = ws.next(("rq", tt_, h))
                        wk, r_wk = ws.next(("rk", tt_, h))
                        wvv, r_wv = ws.next(("rv", tt_, h))
                        wgg, r_wg = ws.next(("rg", tt_, h))
                        wo, r_wo = ws.next(("ro", tt_, h))
                        wqv, wkv = v3(8, 256)(wq), v3(8, 256)(wk)
                        wvv_, wgv_ = v3(8, 512)(wvv), v3(8, 512)(wgg)
                        wov = v3(4, 1024)(wo)
                        if tt_ == 0:
                            kb.I("dve", "memset", [], [r_Sr[0]], S_r[0][:], 0.0)
                        else:
                            acopy(S_r[0][:], S[:, h], [r_S[h]], [r_Sr[0]])
                        gci = 0
                        for (off, n, bi) in blks:
                            smp = (bi == 2)
                            kb.dma("sp", ropec[:, 0:n], rope_c[tt_, :, off:off + n], [], [r_rope])
                            kb.dma("sp", ropes[:, 0:n], rope_s[tt_, :, off:off + n], [], [r_rope])
                            for (wv_, r_w_, dst, r_dst) in ((wqv, r_wq, qT, r_qT), (wkv, r_wk, kT, r_kT)):
                                bb = []
                                for half in range(2):
                                    b = kb.bank()
                                    mm(bankf(b)[:, 0:n],
                                       [(wv_[:, kc, half * P:(half + 1) * P], hn[:, kc, off:off + n]) for kc in range(8)],
                                       [r_w_, r_hn[bi]], [kb.pres[b]])
                                    bb.append(b)
                                c_ = ropec[:, 0:n]
                                s_ = ropes[:, 0:n]
                                p0 = bankf(bb[0])[:, 0:n]
                                p1 = bankf(bb[1])[:, 0:n]
                                t = [x[:, 0:n] for x in tmp]
                                tt(t[0], p0, c_, ALU.mult, [kb.pres[bb[0]], r_rope], [r_tmp])
                                tt(t[1], p1, s_, ALU.mult, [kb.pres[bb[1]], r_rope], [r_tmp])
                                tt(t[2], p0, s_, ALU.mult, [kb.pres[bb[0]], r_rope], [r_tmp])
                                tt(t[3], p1, c_, ALU.mult, [kb.pres[bb[1]], r_rope], [r_tmp])
                                tt(dst[:, 0, 0:n], t[0], t[1], ALU.subtract, [r_tmp], [r_dst])
                                tt(dst[:, 1, 0:n], t[2], t[3], ALU.add, [r_tmp], [r_dst])
                            b = kb.bank()
                            if not smp:
                                items = []
                                for c4 in range(4):
                                    for half in range(2):
                                        items.append((bankb(b)[:, (c4 * 2 + half) * P:(c4 * 2 + half + 1) * P],
                                                      kT[:, half, c4 * P:(c4 + 1) * P], idb[:]))
                                trs(items, [r_kT, r_idb], [kb.pres[b]])
                                act(k_tm[:, :, :], bankb(b)[:, 0:1024].rearrange("p (a b) -> p a b", a=4), AF.Identity,
                                    [kb.pres[b], r_cst], [r_ktm], scale=cst[:, C_KDEC + h:C_KDEC + h + 1])
                            else:
                                items = [(bankb(b)[0:NS, half * P:(half + 1) * P], kT[:, half, 0:NS], idb[:]) for half in range(2)]
                                trs(items, [r_kT, r_idb], [kb.pres[b]])
                                acopy(k_tm[0:NS, 0, :], bankb(b)[0:NS, 0:256], [kb.pres[b]], [r_ktm], scale=1.0 / 16.0)
                            for (wv_, r_w_, dst, r_dst, fn_) in ((wvv_, r_wv, v_tm, r_vt, AF.Copy),
                                                                 (wgv_, r_wg, sg, r_sg, AF.Silu)):
                                for c4 in range(max(1, n // P)):
                                    m = min(P, n)
                                    b = kb.bank()
                                    mm(bankf(b)[0:m, :],
                                       [(hn[:, kc, off + c4 * P: off + c4 * P + m], wv_[:, kc, :]) for kc in range(8)],
                                       [r_w_, r_hn[bi]], [kb.pres[b]])
                                    act(dst[0:m, c4, :], bankf(b)[0:m, :], fn_, [kb.pres[b]], [r_dst])
                            if not smp:
                                for c4 in range(4):
                                    csl = slice(c4 * P, (c4 + 1) * P)
                                    par = gci % 2
                                    gci += 1
                                    ba = kb.bank()
                                    mm(bankf(ba)[:, 0:P], [(kT[:, half, csl], qT[:, half, csl]) for half in range(2)],
                                       [r_kT, r_qT], [kb.pres[ba]])
                                    tt(attT[par], bankf(ba)[:, 0:P], cst[:, C_MASK + h * P:C_MASK + (h + 1) * P], ALU.mult,
                                       [kb.pres[ba], r_cst], [r_att[par]])
                                    bd = []
                                    for half in range(2):
                                        b = kb.bank()
                                        mm(bankf(b), [(k_tm[:, c4, half * P:(half + 1) * P], v_tm[:, c4, :])],
                                           [r_ktm, r_vt], [kb.pres[b]])
                                        bd.append(b)
                                    bo = kb.bank()
                                    mm(bankf(bo), [(attT[par], v_tm[:, c4, :])] +
                                       [(qT[:, half, csl], S_r[par][:, half, :]) for half in range(2)],
                                       [r_att[par], r_vt, r_qT, r_Sr[par]], [kb.pres[bo]])
                                    for half in range(2):
                                        stt(S[:, h, half, :], S[:, h, half, :], cdec, bankf(bd[half]), ALU.mult, ALU.add,
                                            [kb.pres[bd[half]], r_S[h]], [r_S[h]])
                                    acopy(S_r[1 - par][:], S[:, h], [r_S[h]], [r_Sr[1 - par]])
                                    ssq, r_ssq = st(8 + (cnt2[0] % 4))
                                    cnt2[0] += 1
                                    act(junk, bankf(bo), AF.Square, [kb.pres[bo]], [r_junk, r_ssq], accum_out=ssq)
                                    rsqrt_inplace(ssq, [r_cst], r_ssq, 1.0 / 512.0, cst[:, C_EPSP + h:C_EPSP + h + 1])
                                    stt(og[par], bankf(bo), ssq, sg[:, c4, :], ALU.mult, ALU.mult,
                                        [kb.pres[bo], r_ssq, r_sg], [r_og[par]])
                                    bt = kb.bank()
                                    trs([(bankb(bt)[:, e4 * P:(e4 + 1) * P], og[par][:, e4 * P:(e4 + 1) * P], idb[:]) for e4 in range(4)],
                                        [r_og[par], r_idb], [kb.pres[bt]])
                                    anycopy(ogT[:, :, csl], bankb(bt)[:, 0:512].rearrange("p (a b) -> p a b", a=4),
                                            [kb.pres[bt]], [r_ogT])
                            else:
                                tt(prod, qT[:, :, 0:NS], kT[:, :, 0:NS], ALU.mult, [r_qT, r_kT], [r_prod])
                                b = kb.bank()
                                mm(bankf(b)[0:NS, 0:1], [(prod[:, half, :], onesb[:, 0:1]) for half in range(2)],
                                   [r_prod, r_ones], [kb.pres[b]])
                                att_s, r_atts = st(32, NS)
                                ts(att_s, bankf(b)[0:NS, 0:1], 1.0 / 16.0, None, ALU.mult, None, [kb.pres[b]], [r_atts])
                                ts(qmr[:].rearrange("p a b c -> p (a b c)"), cst[:, 0:2 * NS * NS], 0.0, None, ALU.mult, None,
                                   [r_cst], [r_qm])
                                for half in range(2):
                                    for bb_ in range(NS):
                                        vcopy(qmr[:, half, bb_, bb_:bb_ + 1], qT[:, half, bb_:bb_ + 1], [r_qT], [r_qm])
                                bc = kb.pin()
                                for bb_ in range(NS):
                                    kb.dma("pool", S0b[:],
                                           ret_s0[bb_, h].rearrange("(k p) e -> p k e", p=P), [], [r_S0])
                                    mm(bankf(bc)[0:NS, :],
                                       [(qmr[:, half, bb_, :], S0b[:, half, :]) for half in range(2)],
                                       [r_qm, r_S0], [kb.pres[bc]], start=(bb_ == 0), stop=(bb_ == NS - 1))
                                    sp_ = bb_ % 2
                                    ts(kmask[sp_][0:NS, :], k_tm[0:NS, 0, :], cident[0:NS, bb_:bb_ + 1], None, ALU.mult, None,
                                       [r_ktm, r_cst], [r_km[sp_]])
                                    for half in range(2):
                                        b = kb.bank()
                                        mm(bankf(b), [(kmask[sp_][0:NS, half * P:(half + 1) * P], v_tm[0:NS, 0, :])],
                                           [r_km[sp_], r_vt], [kb.pres[b]])
                                        stt(Snew[half], S0b[:, half, :].bitcast(F32), g_h, bankf(b), ALU.mult, ALU.add,
                                            [kb.pres[b], r_S0], [r_Sn[half]])
                                        out_ops.append(kb.dma("sp", o_ret_s[bb_, h, half * P:(half + 1) * P, :], Snew[half],
                                                              [r_Sn[half]], []))
                                act(tcross[0:NS, :], bankf(bc)[0:NS, :], AF.Copy, [kb.pres[bc]], [r_tc], scale=g_h)
                                kb.unpin(bc)
                                stt(o_s[0:NS, :], v_tm[0:NS, 0, :], att_s, tcross[0:NS, :], ALU.mult, ALU.add,
                                    [r_vt, r_atts, r_tc], [r_os])
                                ssq, r_ssq = st(33, NS)
                                act(junk[0:NS, :], o_s[0:NS, :], AF.Square, [r_os], [r_junk, r_ssq], accum_out=ssq)
                                rsqrt_inplace(ssq, [], r_ssq, 1.0 / 512.0, EPS)
                                stt(ogs[0:NS, :], o_s[0:NS, :], ssq, sg[0:NS, 0, :], ALU.mult, ALU.mult,
                                    [r_os, r_ssq, r_sg], [r_ogs])
                                bt = kb.bank()
                                trs([(bankb(bt)[:, e4 * NS:(e4 + 1) * NS], ogs[0:NS, e4 * P:(e4 + 1) * P], idb[0:NS, 0:NS]) for e4 in range(4)],
                                    [r_ogs, r_idb], [kb.pres[bt]])
                                anycopy(ogT[:, :, 0:NS], bankb(bt)[:, 0:4 * NS].rearrange("p (a b) -> p a b", a=4),
                                        [kb.pres[bt]], [r_ogT])
                            for fc in range(8):
                                b = kb.bank()
                                mm(bankf(b)[:, 0:n], [(wov[:, e4, fc * P:(fc + 1) * P], ogT[:, e4, 0:n]) for e4 in range(4)],
                                   [r_wo, r_ogT], [kb.pres[b]])
                                resid_add(b, fc, off, n, bi)
                        ws.release(("rq", tt_, h), ("rk", tt_, h), ("rv", tt_, h), ("rg", tt_, h), ("ro", tt_, h))
                    if last:
                        out_ops.append(kb.dma("sp", o_ret_p.rearrange("h (k p) e -> p h k e", p=P), S[:], r_S, []))
                else:
                    y = carve([8, TT], BF16)
                    r_y = [ares() for _ in range(3)]
                    xbh = carve([2, 3 + 512], F32)
                    r_xbh = ares()
                    xc = carve([2, 512], F32)
                    xcb = carve([2, 512], BF16)
                    ra = carve([2, 512], F32)
                    iu = carve([2, 512], F32)
                    t1 = carve([2, 512], F32)
                    gel = PH.sq.rearrange("p a b -> p (a b)").bitcast(F32)[:, 0:1024].rearrange("p (a b) -> p a b", a=2)
                    r_gel = PH.r_sq
                    r_xc, r_xcb, r_ra, r_iu, r_t1 = (ares() for _ in range(5))
                    wav = carve([4, 2, 256], BF16)
                    wxv = carve([4, 2, 256], BF16)
                    r_wgt = ares()
                    kb.dma("pool", wav, lru_wa.rearrange("b (k p) o -> p b k o", p=P), [], [r_wgt])
                    kb.dma("pool", wxv, lru_wx.rearrange("b (k p) o -> p b k o", p=P), [], [r_wgt])
                    for j in range(4):
                        wx_, r_wx = ws.next(("lx", tt_, j))
                        wgb_, r_wgb = ws.next(("lgb", tt_, j))
                        wxv_ = v3(8, 256)(wx_)
                        wgbv_ = v3(8, 256)(wgb_)
                        for (off, n, bi) in blks:
                            smp = (bi == 2)
                            for c2 in range(2):
                                fo = 2 * j + c2
                                b = kb.bank()
                                mm(bankf(b)[:, 0:n], [(wxv_[:, kc, c2 * P:(c2 + 1) * P], hn[:, kc, off:off + n]) for kc in range(8)],
                                   [r_wx, r_hn[bi]], [kb.pres[b]])
                                w3 = vec[:, V_LCW + 24 + fo:V_LCW + 24 + fo + 1]
                                w2 = vec[:, V_LCW + 16 + fo:V_LCW + 16 + fo + 1]
                                w1 = vec[:, V_LCW + 8 + fo:V_LCW + 8 + fo + 1]
                                w0 = vec[:, V_LCW + fo:V_LCW + fo + 1]
                                cb_ = vec[:, V_LCB + fo:V_LCB + fo + 1]
                                xco = xc[:, c2, 0:n]
                                if not smp:
                                    acopy(xbh[:, c2, 3:3 + n], bankf(b)[:, 0:n], [kb.pres[b]], [r_xbh])
                                    vcopy(xbh[:, c2, 0:3], lru_halo[:, fo, :], [r_lhalo[fo]], [r_xbh])
                                    vcopy(lru_halo[:, fo, :], xbh[:, c2, n:n + 3], [r_xbh], [r_lhalo[fo]])
                                    act(xco, xbh[:, c2, 3:3 + n], AF.Identity, [r_xbh, r_vec], [r_xc], scale=w3, bias=cb_)
                                    taps = [xbh[:, c2, 2:2 + n], xbh[:, c2, 1:1 + n], xbh[:, c2, 0:n]]
                                    r_taps = r_xbh
                                else:
                                    acopy(xbs_keep[:, fo, :], bankf(b)[:, 0:n], [kb.pres[b]], [r_xbs])
                                    act(xco, bankf(b)[:, 0:n], AF.Identity, [kb.pres[b], r_vec], [r_xc], scale=w3, bias=cb_)
                                    taps = [cvs[:, fo, :, 2], cvs[:, fo, :, 1], cvs[:, fo, :, 0]]
                                    r_taps = r_cvs
                                for tap_ap, wcol in zip(taps, (w2, w1, w0)):
                                    stt(xco, tap_ap, wcol, xco, ALU.mult, ALU.add, [r_taps, r_xc, r_vec], [r_xc])
                                acopy(xcb[:, c2, 0:n], xco, [r_xc], [r_xcb])
                            for c2 in range(2):
                                fo = 2 * j + c2
                                osl = slice(c2 * P, (c2 + 1) * P)
                                br = kb.bank()
                                mm(bankf(br)[:, 0:n], [(wav[:, j, k2, osl], xcb[:, k2, 0:n]) for k2 in range(2)],
                                   [r_wgt, r_xcb], [kb.pres[br]])
                                bi_ = kb.bank()
                                mm(bankf(bi_)[:, 0:n], [(wxv[:, j, k2, osl], xcb[:, k2, 0:n]) for k2 in range(2)],
                                   [r_wgt, r_xcb], [kb.pres[bi_]])
                                bg = kb.bank()
                                mm(bankf(bg)[:, 0:n], [(wgbv_[:, kc, osl], hn[:, kc, off:off + n]) for kc in range(8)],
                                   [r_wgb, r_hn[bi]], [kb.pres[bg]])
                                a_ = ra[:, c2, 0:n]
                                i_ = iu[:, c2, 0:n]
                                t_ = t1[:, c2, 0:n]
                                g_ = gel[:, c2, 0:n]
                                xco = xc[:, c2, 0:n]
                                act(a_, bankf(br)[:, 0:n], AF.Sigmoid, [kb.pres[br], r_vec], [r_ra],
                                    bias=vec[:, V_LBA + fo:V_LBA + fo + 1])
                                act(i_, bankf(bi_)[:, 0:n], AF.Sigmoid, [kb.pres[bi_], r_vec], [r_iu],
                                    bias=vec[:, V_LBX + fo:V_LBX + fo + 1])
                                act(a_, a_, AF.Exp, [r_ra, r_c8], [r_ra], scale=c8[:, fo:fo + 1])
                                act(t_, a_, AF.Square, [r_ra], [r_t1])
                                act(t_, t_, AF.Sqrt, [r_t1], [r_t1], scale=-1.0, bias=1.0)
                                act(g_, bankf(bg)[:, 0:n], AF.Gelu_apprx_tanh, [kb.pres[bg]], [r_gel])
                                tt(i_, i_, xco, ALU.mult, [r_iu, r_xc], [r_iu])
                                tt(i_, i_, t_, ALU.mult, [r_iu, r_t1], [r_iu])
                                if not smp:
                                    kb.I("dve", "tensor_tensor_scan", [r_ra, r_iu, r_hc[fo]], [r_t1],
                                         t_, a_, i_, hcarry[:, fo:fo + 1], ALU.mult, ALU.add)
                                    vcopy(hcarry[:, fo:fo + 1], t1[:, c2, n - 1:n], [r_t1], [r_hc[fo]])
                                else:
                                    tt(t_, a_, h0s[:, fo, :], ALU.mult, [r_ra, r_h0s], [r_t1])
                                    tt(t_, t_, i_, ALU.add, [r_t1, r_iu], [r_t1])
                                    vcopy(hs_keep[:, fo, :], t_, [r_t1], [r_hsk])
                                tt(y[:, fo, off:off + n], g_, t_, ALU.mult, [r_gel, r_t1], [r_y[bi]])
                        ws.release(("lx", tt_, j), ("lgb", tt_, j))
                    for j in range(2):
                        w, r_w = ws.next(("lo", tt_, j))
                        lin_fm(r_w, v3(8, 512)(w), 4, y, lambda bi: [r_y[bi]], blks,
                               lambda b, fo, off, n, bi, j=j: resid_add(b, 4 * j + fo, off, n, bi))
                        ws.release(("lo", tt_, j))

                phase()
                norm(xT, r_x, V_NXA + 8 * l, blks, hn, r_hn)
                qx = carve([8, TT], BF16)
                r_qx = [[ares() for _ in range(H)] for _ in range(3)]
                pf = [carve([MEM], F32) for _ in range(2)]
                pn = [carve([MEM], BF16) for _ in range(2)]
                pT = [carve([2, 512], BF16) for _ in range(2)]
                r_pf = [ares() for _ in range(2)]
                r_pn = [ares() for _ in range(2)]
                r_pT = [ares() for _ in range(2)]
                for j in range(2):
                    w, r_w = ws.next(("xq", tt_, l, j))
                    lin_fm(r_w, v3(8, 512)(w), 4, hn, lambda bi: [r_hn[bi]], blks,
                           lambda b, fo, off, n, bi, j=j: anycopy(qx[:, 4 * j + fo, off:off + n], bankf(b)[:, 0:n],
                                                                  [kb.pres[b]], [r_qx[bi][(4 * j + fo) // 2]]))
                    ws.release(("xq", tt_, l, j))
                acnt = 0
                for (off, n, bi) in pblks:
                    for h in range(H):
                        bT = kb.bank()
                        for c4 in range(4):
                            csl = slice(off + c4 * P, off + (c4 + 1) * P)
                            k2 = acnt % 2
                            acnt += 1
                            bs = kb.bank()
                            mm(bankf(bs)[:, 0:MEM], [(qx[:, 2 * h + half, csl], kTm[l][:, 2 * h + half, :]) for half in range(2)],
                               [r_qx[bi][h], r_kTm[l]], [kb.pres[bs]])
                            mx, r_mx = st(40 + 2 * k2)
                            den, r_den = st(41 + 2 * k2)
                            kb.I("dve", "reduce_max", [kb.pres[bs]], [r_mx], mx, bankf(bs)[:, 0:MEM], AX.X)
                            ts(mx, mx, -1.0 / 16.0, None, ALU.mult, None, [r_mx], [r_mx])
                            act(pf[k2], bankf(bs)[:, 0:MEM], AF.Exp, [kb.pres[bs], r_mx], [r_pf[k2], r_den],
                                scale=1.0 / 16.0, bias=mx, accum_out=den)
                            kb.I("dve", "reciprocal", [r_den], [r_den], den, den)
                            ts(pn[k2], pf[k2], den, None, ALU.mult, None, [r_pf[k2], r_den], [r_pn[k2]])
                            trs([(bankb(bT)[:, (mc * 4 + c4) * P:(mc * 4 + c4 + 1) * P], pn[k2][:, mc * P:(mc + 1) * P], idb[:])
                                 for mc in range(2)], [r_pn[k2], r_idb], [kb.pres[bT]])
                        pq = h % 2
                        anycopy(pT[pq], bankb(bT)[:, 0:1024].rearrange("p (a b) -> p a b", a=2), [kb.pres[bT]], [r_pT[pq]])
                        for e2 in range(2):
                            b = kb.bank()
                            mm(bankf(b), [(vtm[l][:, mc, h * 256 + e2 * P: h * 256 + (e2 + 1) * P], pT[pq][:, mc, :]) for mc in range(2)],
                               [r_vtm[l], r_pT[pq]], [kb.pres[b]])
                            anycopy(qx[:, 2 * h + e2, off:off + n], bankf(b), [kb.pres[b]], [r_qx[bi][h]])
                if last:
                    kin_all = PH.sq.rearrange("p a b -> p (a b)")
                    kin = [kin_all[:, i * 2048:(i + 1) * 2048].rearrange("p (a b) -> p a b", a=2) for i in range(2)]
                    kTs = [carve([8, MEM], BF16) for _ in range(2)]
                    vs = [carve([2, D], BF16) for _ in range(2)]
                    r_kin = [PH.r_sq, PH.r_sq]
                    r_kTs = [ares() for _ in range(2)]
                    r_vs = [ares() for _ in range(2)]
                    sTs = carve([2, 64], F32)
                    s_tm = carve([MEM], F32)
                    p_tm = carve([MEM], BF16)
                    pTs = carve([2, 64], BF16)
                    r_sTs, r_stm, r_ptm, r_pTs = (ares() for _ in range(4))
                    bS = kb.pin()
                    ts(bankf(bS)[:, 0:P], cident, 0.0, None, ALU.mult, None, [r_cst], [kb.pres[bS]])
                    for bb_ in range(NS):
                        k2 = bb_ % 2
                        for hf in range(2):
                            cs = slice(hf * 512, (hf + 1) * 512)
                            kb.dma("pool", kin[k2][:, :, cs], mem_k[l, bb_].rearrange("(k p) f -> p k f", p=P)[:, :, cs], [], [r_kin[k2]])
                            kb.dma("pool", vs[k2][:, :, cs], mem_v[l, bb_].rearrange("(k p) f -> p k f", p=P)[:, :, cs], [], [r_vs[k2]])
                        for f0 in (0, 4):
                            b = kb.bank()
                            items = []
                            for f in range(4):
                                for mc in range(2):
                                    items.append((bankb(b)[:, (f * 2 + mc) * P:(f * 2 + mc + 1) * P],
                                                  kin[k2][:, mc, (f0 + f) * P:(f0 + f + 1) * P], idb[:]))
                            trs(items, [r_kin[k2], r_idb], [kb.pres[b]])
                            anycopy(kTs[k2][:, f0:f0 + 4, :], bankb(b)[:, 0:1024].rearrange("p (a b) -> p a b", a=4),
                                    [kb.pres[b]], [r_kTs[k2]])
                        for h in range(H):
                            for mc in range(2):
                                col = mc * 64 + h * NS + bb_
                                mm(bankf(bS)[:, col:col + 1],
                                   [(kTs[k2][:, 2 * h + half, mc * P:(mc + 1) * P], qx[:, 2 * h + half, T + bb_:T + bb_ + 1])
                                    for half in range(2)], [r_kTs[k2], r_qx[2][h]], [kb.pres[bS]])
                        if k2 == 1:
                            vcopy(sTs, bankf(bS)[:, 0:P].rearrange("p (a b) -> p a b", a=2), [kb.pres[bS]], [r_sTs])
                            b = kb.bank()
                            trs([(bankf(b)[0:64, mc * P:(mc + 1) * P], sTs[:, mc, :], cident) for mc in range(2)],
                                [r_sTs, r_cst], [kb.pres[b]])
                            mx, r_mx = st(48, 64)
                            den, r_den = st(49, 64)
                            kb.I("dve", "reduce_max", [kb.pres[b]], [r_mx], mx, bankf(b)[0:64, 0:MEM], AX.X)
                            ts(mx, mx, -1.0 / 16.0, None, ALU.mult, None, [r_mx], [r_mx])
                            act(s_tm[0:64, :], bankf(b)[0:64, 0:MEM], AF.Exp, [kb.pres[b], r_mx], [r_stm, r_den],
                                scale=1.0 / 16.0, bias=mx, accum_out=den)
                            kb.I("dve", "reciprocal", [r_den], [r_den], den, den)
                            ts(p_tm[0:64, :], s_tm[0:64, :], den, None, ALU.mult, None, [r_stm, r_den], [r_ptm])
                            b2 = kb.bank()
                            trs([(bankb(b2)[:, mc * 64:(mc + 1) * 64], p_tm[0:64, mc * P:(mc + 1) * P], idb[0:64, 0:64]) for mc in range(2)],
                                [r_ptm, r_idb], [kb.pres[b2]])
                            anycopy(pTs, bankb(b2)[:, 0:P].rearrange("p (a b) -> p a b", a=2), [kb.pres[b2]], [r_pTs])
                            bO = kb.bank()
                            for sb_ in (bb_ - 1, bb_):
                                kk = sb_ % 2
                                for h in range(H):
                                    for e2 in range(2):
                                        col = (sb_ - (bb_ - 1)) * 8 + 2 * h + e2
                                        mm(bankf(bO)[:, col:col + 1],
                                           [(vs[kk][:, mc, h * 256 + e2 * P:h * 256 + (e2 + 1) * P],
                                             pTs[:, mc, h * NS + sb_:h * NS + sb_ + 1]) for mc in range(2)],
                                           [r_vs[kk], r_pTs], [kb.pres[bO]])
                            for s2 in range(2):
                                anycopy(qx[:, :, T + bb_ - 1 + s2:T + bb_ + s2], bankf(bO)[:, 8 * s2:8 * s2 + 8].rearrange("p (f o) -> p f o", o=1),
                                        [kb.pres[bO]], r_qx[2])
                    kb.unpin(bS)
                for j in range(2):
                    w, r_w = ws.next(("xo", tt_, l, j))
                    lin_fm(r_w, v3(8, 512)(w), 4, qx, lambda bi: r_qx[bi], blks,
                           lambda b, fo, off, n, bi, j=j: resid_add(b, 4 * j + fo, off, n, bi))
                    ws.release(("xo", tt_, l, j))

                phase()
                norm(xT, r_x, V_NFFN + 8 * l, blks, hn, r_hn)
                yh = carve([12, TT], BF16)
                r_yh = [ares() for _ in range(3)]
                ub0 = carve([4, 2 + 512], F32)
                ub = [ub0, ub0]
                ucA = carve([4, 512], F32)
                uc = [ucA, ucA]
                r_ubf = [ares() for _ in range(4)]
                r_ub = [r_ubf, r_ubf]
                r_ucf = [ares() for _ in range(4)]
                r_uc = [r_ucf, r_ucf]
                fcnt = 0
                for hh in range(2):
                    for j3 in range(3):
                        j = 3 * hh + j3
                        wu, r_wu = ws.next(("fu", tt_, l, j))
                        wgt_, r_wgt_ = ws.next(("fg", tt_, l, j))
                        wuv = v3(8, 512)(wu)
                        wgv = v3(8, 512)(wgt_)
                        for (off, n, bi) in blks:
                            pb = fcnt % 2
                            fcnt += 1
                            smp = (bi == 2)
                            for f4 in range(4):
                                fo = 4 * j + f4
                                bu = kb.bank()
                                mm(bankf(bu)[:, 0:n], [(wuv[:, kc, f4 * P:(f4 + 1) * P], hn[:, kc, off:off + n]) for kc in range(8)],
                                   [r_wu, r_hn[bi]], [kb.pres[bu]])
                                bg = kb.bank()
                                mm(bankf(bg)[:, 0:n], [(wgv[:, kc, f4 * P:(f4 + 1) * P], hn[:, kc, off:off + n]) for kc in range(8)],
                                   [r_wgt_, r_hn[bi]], [kb.pres[bg]])
                                cw = lambda t_, fo=fo: vec[:, V_FCW + (l * 3 + t_) * 24 + fo: V_FCW + (l * 3 + t_) * 24 + fo + 1]
                                cbias = vec[:, V_FCB + l * 24 + fo:V_FCB + l * 24 + fo + 1]
                                uco = uc[pb][:, f4, 0:n]
                                if not smp:
                                    uh = ub[pb]
                                    acopy(uh[:, f4, 2:2 + n], bankf(bu)[:, 0:n], [kb.pres[bu]], [r_ub[pb][f4]])
                                    vcopy(uh[:, f4, 0:2], ffn_halo[l][:, fo, :], [r_fhalo[l][fo]], [r_ub[pb][f4]])
                                    vcopy(ffn_halo[l][:, fo, :], uh[:, f4, n:n + 2], [r_ub[pb][f4]], [r_fhalo[l][fo]])
                                    act(uco, uh[:, f4, 2:2 + n], AF.Identity, [r_ub[pb][f4], r_vec], [r_uc[pb][f4]], scale=cw(2), bias=cbias)
                                    taps = [uh[:, f4, 1:1 + n], uh[:, f4, 0:n]]
                                    r_taps = r_ub[pb][f4]
                                else:
                                    acopy(ukeep[l][:, fo, :], bankf(bu)[:, 0:n], [kb.pres[bu]], [r_uk[l]])
                                    act(uco, bankf(bu)[:, 0:n], AF.Identity, [kb.pres[bu], r_vec], [r_uc[pb][f4]], scale=cw(2), bias=cbias)
                                    taps = [fcs[l][:, fo, :, 1], fcs[l][:, fo, :, 0]]
                                    r_taps = r_fcs[l]
                                for tap_ap, wcol in zip(taps, (cw(1), cw(0))):
                                    stt(uco, tap_ap, wcol, uco, ALU.mult, ALU.add, [r_taps, r_uc[pb][f4], r_vec], [r_uc[pb][f4]])
                                act(uco, uco, AF.Gelu_apprx_tanh, [r_uc[pb][f4]], [r_uc[pb][f4]])
                                tt(yh[:, 4 * j3 + f4, off:off + n], bankf(bg)[:, 0:n], uco, ALU.mult,
                                   [kb.pres[bg], r_uc[pb][f4]], [r_yh[bi]])
                        ws.release(("fu", tt_, l, j), ("fg", tt_, l, j))
                    wd = []
                    for j3 in range(3):
                        j = 3 * hh + j3
                        w, r_w = ws.next(("fd", tt_, l, j))
                        wd.append((v3(4, 1024)(w), r_w))
                    for (off, n, bi) in blks:
                        for fc in range(8):
                            b = kb.bank()
                            mm(bankf(b)[:, 0:n],
                               [(wd[j3][0][:, f4, fc * P:(fc + 1) * P], yh[:, 4 * j3 + f4, off:off + n])
                                for j3 in range(3) for f4 in range(4)],
                               [wd[0][1], wd[1][1], wd[2][1], r_yh[bi]], [kb.pres[b]])
                            resid_add(b, fc, off, n, bi)
                    ws.release(*[("fd", tt_, l, 3 * hh + j3) for j3 in range(3)])

            phase(stage=True)
            fin = carve([8, 512], F32)
            r_fin = ares()
            for (off, n, bi) in blks:
                norm(xT, r_x, V_NFIN, [(off, n, bi)], _Shift(fin, off), [r_fin] * 3)
                if bi < 2:
                    for c4 in range(4):
                        fm_to_tm(lambda fc, c4=c4: fin[:, fc, c4 * P:(c4 + 1) * P], [r_fin], P, 8,
                                 y_p[tok0 + off + c4 * P: tok0 + off + (c4 + 1) * P, :])
                else:
                    fm_to_tm(lambda fc: fin[:, fc, 0:NS], [r_fin], NS, 8, y_s)

        phase(stage=True, sq_=False)
        b = kb.bank()
        trs([(bankf(b)[0:8, 0:P], hcarry[:], cident)], r_hc + [r_cst], [kb.pres[b]])
        t_h, r_th = stage_buf()
        acopy(t_h[0:8, 0:P], bankf(b)[0:8, 0:P], [kb.pres[b]], [r_th])
        out_ops.append(kb.dma("sp", o_lruh_p.rearrange("o (c q) -> (o c) q", q=P), t_h[0:8, 0:P], [r_th], []))
        b = kb.bank()
        trs([(bankf(b)[0:24, 0:P], lru_halo[:].rearrange("p a b -> p (a b)"), cident)], r_lhalo + [r_cst], [kb.pres[b]])
        t_c, r_tc2 = stage_buf()
        acopy(t_c[0:24, 0:P], bankf(b)[0:24, 0:P], [kb.pres[b]], [r_tc2])
        for fc in range(8):
            out_ops.append(kb.dma("sp", o_lrucv_p[:, fc * P:(fc + 1) * P], t_c[3 * fc:3 * fc + 3, 0:P], [r_tc2], []))
        for l in range(2):
            b = kb.bank()
            trs([(bankf(b)[0:48, 0:P], ffn_halo[l][:].rearrange("p a b -> p (a b)"), cident)], r_fhalo[l] + [r_cst], [kb.pres[b]])
            t_f, r_tf = stage_buf()
            acopy(t_f[0:48, 0:P], bankf(b)[0:48, 0:P], [kb.pres[b]], [r_tf])
            for fo in range(24):
                out_ops.append(kb.dma("sp", o_ffncv_p[l, :, fo * P:(fo + 1) * P], t_f[2 * fo:2 * fo + 2, 0:P], [r_tf], []))
        fm_to_tm(lambda fc: hs_keep[:, fc, :], [r_hsk], NS, 8, o_lruh_s)
        fm_to_tm(lambda fc: xbs_keep[:, fc, :], [r_xbs], NS, 8, o_lrucv_s[:, 2, :])
        out_ops.append(kb.dma("sp", o_lrucv_s[:, 0:2, :], lru_cv0[:, 1:3, :], [], []))
        for l in range(2):
            fm_to_tm(lambda fc, l=l: ukeep[l][:, fc, :], [r_uk[l]], NS, 24, o_ffncv_s[l, :, 1, :])
            out_ops.append(kb.dma("sp", o_ffncv_s[l, :, 0, :], ffn_cv0[l, :, 1, :], [], []))


    import os as _os
    try:
        record_all()
    except _StopBuild:
        pass

    fin_res = []
    lastd = {}
    for e in ENG:
        for o in kb.ops[e]:
            if o.dma:
                lastd[o.dsem] = o
    lastc = [kb.ops[e][-1] for e in ("pe", "act", "dve", "pool") if kb.ops[e] and not kb.ops[e][-1].dma]
    for o in list(out_ops) + list(lastd.values()) + lastc:
        r = Res()
        r.w = o
        fin_res.append(r)
    kb.op("sp", lambda e: e.nop(), rd=fin_res, wr=[])
    assert PH.stopped or ws.used == len(ws.pieces), (ws.used, len(ws.pieces))
    kb.emit(es)
    return nc, es


class _Shift:
    def __init__(self, base, off):
        self.base = base
        self.off = off

    def __getitem__(self, key):
        p, fc, sl = key
        return self.base[p, fc, sl.start - self.off: sl.stop - self.off]


def _fm(v):
    v = np.asarray(v, np.float32)
    return np.ascontiguousarray(v.reshape(-1, P).T)


def _consts():
    cst = np.zeros((P, NC), np.float32)
    cst[:, C_ID:C_ID + P] = np.eye(P, dtype=np.float32)
    j = np.arange(P, dtype=np.float64)
    for h in range(H):
        g = 1.0 - 2.0 ** (-5.0 - h)
        m = np.zeros((P, P), np.float64)
        m[:, :] = (g ** (-(j[:, None] + 1.0))) / 16.0
        m = np.where(j[None, :] >= j[:, None], m, 0.0)
        cst[:, C_MASK + h * P:C_MASK + (h + 1) * P] = m.astype(np.float32)
        cst[:, C_KDEC + h] = (g ** (127.0 - j) / 16.0).astype(np.float32)
        cst[:, C_EPSP + h] = (EPS / (g ** (2.0 * (j + 1.0)))).astype(np.float32)
    half = 128
    inv = (np.float32(10000.0) ** (-(np.arange(half, dtype=np.float32) / np.float32(half)))).astype(np.float32)
    rc = np.zeros((NT, P, TT), np.float32)
    rsn = np.zeros((NT, P, TT), np.float32)
    for t in range(NT):
        pos = np.concatenate([np.arange(t * T, (t + 1) * T), np.full(NS, 16384)]).astype(np.float32)
        ang = (pos[None, :] * inv[:, None]).astype(np.float32).astype(np.float64)
        rc[t] = np.cos(ang).astype(np.float32)
        rsn[t] = np.sin(ang).astype(np.float32)
    return cst, rc, rsn


_CACHE = {}


def kernel(x_prompt, x_sample, state_ret, state_lru_h, state_lru_conv, state_ffn_conv,
           cache_mem_k, cache_mem_v, mem_prompt, norm_mix, norm_xa, norm_mem, norm_ffn, norm_final,
           ret_w_in, ret_w_out, lru_w_in, lru_conv_w, lru_conv_b, lru_wa, lru_ba, lru_wx, lru_bx,
           lru_lambda, lru_w_out, xa_w_q, xa_w_kv, xa_w_o, ffn_w_up, ffn_conv_w, ffn_conv_b, ffn_w_down):
    f = lambda a: np.ascontiguousarray(np.asarray(a, dtype=np.float32))
    n = 8
    vec = np.zeros((P, NV), np.float32)
    for l in range(2):
        vec[:, V_NMIX + 8 * l:V_NMIX + 8 * l + 8] = _fm(norm_mix[l])
        vec[:, V_NXA + 8 * l:V_NXA + 8 * l + 8] = _fm(norm_xa[l])
        vec[:, V_NMEM + 8 * l:V_NMEM + 8 * l + 8] = _fm(norm_mem[l])
        vec[:, V_NFFN + 8 * l:V_NFFN + 8 * l + 8] = _fm(norm_ffn[l])
        for t in range(3):
            vec[:, V_FCW + (l * 3 + t) * 24:V_FCW + (l * 3 + t + 1) * 24] = _fm(np.asarray(ffn_conv_w)[l, t])
        vec[:, V_FCB + l * 24:V_FCB + (l + 1) * 24] = _fm(np.asarray(ffn_conv_b)[l])
    vec[:, V_NFIN:V_NFIN + 8] = _fm(norm_final)
    for t in range(4):
        vec[:, V_LCW + 8 * t:V_LCW + 8 * t + 8] = _fm(np.asarray(lru_conv_w)[0, t])
    vec[:, V_LCB:V_LCB + 8] = _fm(np.asarray(lru_conv_b)[0])
    vec[:, V_LBA:V_LBA + 8] = _fm(np.asarray(lru_ba)[0].reshape(-1))
    vec[:, V_LBX:V_LBX + 8] = _fm(np.asarray(lru_bx)[0].reshape(-1))
    vec[:, V_LLAM:V_LLAM + 8] = _fm(np.asarray(lru_lambda)[0])
    cst, rc, rsn = _consts()
    shared = {
        "vec": vec, "cst": cst, "rope_c": rc, "rope_s": rsn,
        "ret_w_in": f(ret_w_in)[0], "ret_w_out": f(ret_w_out)[0], "lru_w_in": f(lru_w_in)[0],
        "lru_wa": f(lru_wa)[0], "lru_wx": f(lru_wx)[0], "lru_w_out": f(lru_w_out)[0],
        "xa_w_q": f(xa_w_q), "xa_w_kv": f(xa_w_kv), "xa_w_o": f(xa_w_o),
        "ffn_w_up": f(ffn_w_up), "ffn_w_down": f(ffn_w_down),
    }
    x_prompt = f(x_prompt); x_sample = f(x_sample); state_ret = f(state_ret); state_lru_h = f(state_lru_h)
    state_lru_conv = f(state_lru_conv); state_ffn_conv = f(state_ffn_conv)
    cache_mem_k = f(cache_mem_k); cache_mem_v = f(cache_mem_v); mem_prompt = f(mem_prompt)
    in_maps = []
    for c in range(n):
        s = slice(NS * c, NS * (c + 1))
        m = dict(shared)
        m.update({
            "x_p": x_prompt[c], "x_s": x_sample[s, 0, :], "ret_s0": state_ret[0, s],
            "lru_h0": state_lru_h[0, s], "lru_cv0": state_lru_conv[0, s],
            "ffn_cv0": np.ascontiguousarray(state_ffn_conv[:, s]),
            "mem_k": np.ascontiguousarray(cache_mem_k[:, s].reshape(2, NS, MEM, D)),
            "mem_v": np.ascontiguousarray(cache_mem_v[:, s].reshape(2, NS, MEM, D)),
            "mem_p": mem_prompt[c],
        })
        in_maps.append(m)
    if "nc" not in _CACHE:
        _CACHE["nc"] = build_program()
    nc, _es = _CACHE["nc"]
    res = run_bass_kernel_spmd(nc, in_maps, core_ids=list(range(n)))
    R_ = res.results
    cat = lambda k: np.concatenate([np.asarray(r[k]) for r in R_], axis=0)
    stk = lambda k: np.stack([np.asarray(r[k]) for r in R_], axis=0)
    y_prompt = stk("y_p")
    y_sample = cat("y_s").reshape(n * NS, 1, D)
    new_ret_p = stk("o_ret_p")[None]
    new_ret_s = cat("o_ret_s")[None]
    new_lru_h_p = cat("o_lruh_p")[None]
    new_lru_h_s = cat("o_lruh_s")[None]
    new_lru_conv_p = stk("o_lrucv_p")[None]
    new_lru_conv_s = cat("o_lrucv_s")[None]
    new_ffn_conv_p = np.stack([np.asarray(r["o_ffncv_p"]) for r in R_], axis=1)
    new_ffn_conv_s = np.concatenate([np.asarray(r["o_ffncv_s"]) for r in R_], axis=1)
    new_mem_k_p = np.stack([np.asarray(r["o_memk_p"]) for r in R_], axis=1).reshape(2, n, MEM, 4, 256)
    new_mem_v_p = np.stack([np.asarray(r["o_memv_p"]) for r in R_], axis=1).reshape(2, n, MEM, 4, 256)
    outs = (y_prompt, y_sample, new_ret_p, new_ret_s, new_lru_h_p, new_lru_h_s, new_lru_conv_p,
            new_lru_conv_s, new_ffn_conv_p, new_ffn_conv_s, new_mem_k_p, new_mem_v_p)
    return tuple(np.ascontiguousarray(o.astype(np.float32)) for o in outs)
```
